# Optimizing a Trainium2 kernel written in Bass

```python
import jax, jax.numpy as jnp
from jax import lax
import numpy as np

D_MODEL = 1024
BATCH = 16
SEQ = 2048
DEPTH = 2
DEC_BATCH = 8
DEC_SEQ = 4096
PAST_LEN = 128

HEAD_DIM = 64
N_MIX_HEADS = D_MODEL // HEAD_DIM
A_HEADS = N_MIX_HEADS // 4
B_HEADS = N_MIX_HEADS // 4
C_HEADS = N_MIX_HEADS // 2
C_KV_HEADS = C_HEADS // 4
A_WIDTH = A_HEADS * HEAD_DIM
B_WIDTH = B_HEADS * HEAD_DIM
C_WIDTH = C_HEADS * HEAD_DIM
C_KV_WIDTH = C_KV_HEADS * HEAD_DIM
DILATED_PAIRS = ((128, 1), (512, 4), (2048, 16))
ROPE_THETA = 500000.0
ROPE_DIMS = HEAD_DIM // 4
AXIAL_THETA = 10000.0
GRID_W = 64
Q_BLOCK = 128
MLSTM_CHUNK = 64
N_MEM = 256
X_HEADS = 4
X_HEAD_DIM = 64
X_WIDTH = X_HEADS * X_HEAD_DIM
D_FF = ((8 * D_MODEL) // 3 + 63) // 64 * 64
CONV_W = 3
RMS_EPS = 1e-6
NEG_INF = -1e30
IN_SPLITS = (A_WIDTH, A_WIDTH, A_WIDTH,
             2 * B_WIDTH, B_WIDTH, B_WIDTH, 4 * B_HEADS,
             C_WIDTH, C_KV_WIDTH, C_KV_WIDTH)
IN_WIDTH = sum(IN_SPLITS)

kernel_name = "hybrid_bidir_encoder_dilated_mlstm_axial_gqa"


def rmsnorm(x, g):
    xf = x.astype(jnp.float32)
    y = xf * lax.rsqrt(jnp.mean(xf * xf, axis=-1, keepdims=True) + RMS_EPS)
    return (y * g.astype(jnp.float32)).astype(x.dtype)


def rope_cos_sin(pos, dim, theta):
    inv = theta ** (-jnp.arange(0, dim, 2, dtype=jnp.float32) / dim)
    ang = pos.astype(jnp.float32)[:, None] * inv[None, :]
    return jnp.cos(ang), jnp.sin(ang)


def rotate(x, cos, sin):
    xf = x.astype(jnp.float32)
    half = xf.shape[-1] // 2
    x1, x2 = xf[..., :half], xf[..., half:]
    c, s = cos[None, :, None, :], sin[None, :, None, :]
    return jnp.concatenate([x1 * c - x2 * s, x1 * s + x2 * c], axis=-1).astype(x.dtype)


def partial_rope(x, rope):
    return jnp.concatenate([rotate(x[..., :ROPE_DIMS], *rope), x[..., ROPE_DIMS:]], axis=-1)


def axial_rope(x, rope_row, rope_col):
    half = HEAD_DIM // 2
    return jnp.concatenate([rotate(x[..., :half], *rope_row), rotate(x[..., half:], *rope_col)], axis=-1)


def dwconv3(x, w, b):
    xp = jnp.pad(x, ((0, 0), (1, 1), (0, 0)))
    return xp[:, :-2] * w[0] + xp[:, 1:-1] * w[1] + xp[:, 2:] * w[2] + b


def dilated_branch(q, k, v, window, dilation):
    B, S, H, Dh = q.shape
    R = window // (2 * dilation)
    L = S // dilation
    nblk = -(-L // R)
    Lp = nblk * R

    def strided(t):
        return t.reshape(B, L, dilation, H, Dh).transpose(0, 2, 1, 3, 4)

    qb = jnp.pad(strided(q), ((0, 0), (0, 0), (0, Lp - L), (0, 0), (0, 0))).reshape(B, dilation, nblk, R, H, Dh)
    pad_k = ((0, 0), (0, 0), (R, Lp - L + R), (0, 0), (0, 0))
    ks = jnp.pad(strided(k), pad_k).reshape(B, dilation, nblk + 2, R, H, Dh)
    vs = jnp.pad(strided(v), pad_k).reshape(B, dilation, nblk + 2, R, H, Dh)
    band = lambda t: jnp.concatenate([t[:, :, :-2], t[:, :, 1:-1], t[:, :, 2:]], axis=3)
    kw, vw = band(ks), band(vs)
    qpos = jnp.arange(nblk)[:, None] * R + jnp.arange(R)[None, :]
    kpos = jnp.arange(nblk)[:, None] * R - R + jnp.arange(3 * R)[None, :]
    kp = kpos[:, None, :]
    mask = (jnp.abs(kp - qpos[:, :, None]) <= R) & (kp >= 0) & (kp < L)
    s = jnp.einsum('bdnqhe,bdnkhe->bdnhqk', qb, kw).astype(jnp.float32)
    s = jnp.where(mask[:, None], s, NEG_INF)
    lse = jax.nn.logsumexp(s, axis=-1)
    p = jnp.exp(s - lse[..., None]).astype(v.dtype)
    o = jnp.einsum('bdnhqk,bdnkhe->bdnqhe', p, vw)
    o = o.reshape(B, dilation, Lp, H, Dh)[:, :, :L].transpose(0, 2, 1, 3, 4).reshape(B, S, H, Dh)
    lse = lse.transpose(0, 1, 2, 4, 3).reshape(B, dilation, Lp, H)[:, :, :L].transpose(0, 2, 1, 3).reshape(B, S, H)
    return o, lse


def dilated_attention(q, k, v):
    outs, lses = [], []
    for window, dilation in DILATED_PAIRS:
        o, lse = dilated_branch(q, k, v, window, dilation)
        outs.append(o)
        lses.append(lse)
    wts = jax.nn.softmax(jnp.stack(lses, axis=0), axis=0)
    y = jnp.einsum('nbsh,nbshd->bshd', wts, jnp.stack(outs, axis=0).astype(jnp.float32))
    return y.astype(v.dtype)


def mlstm_chunkwise(q, k, v, ig, lf):
    N, H, S, Dh = q.shape
    nc = S // MLSTM_CHUNK

    def chunked(t):
        return jnp.moveaxis(t.reshape((N, H, nc, MLSTM_CHUNK) + t.shape[3:]), 2, 0)

    causal = jnp.tril(jnp.ones((MLSTM_CHUNK, MLSTM_CHUNK), dtype=bool))

    def step(carry, inp):
        C, n, m = carry
        qc, kc, vc, igc, lfc = inp
        b = jnp.cumsum(lfc, axis=-1)
        a = b + m[..., None]
        logw = jnp.where(causal, b[..., :, None] - b[..., None, :] + igc[..., None, :], NEG_INF)
        m_loc = jnp.maximum(a, jnp.max(logw, axis=-1))
        w_state = jnp.exp(a - m_loc)
        w_intra = jnp.exp(logw - m_loc[..., None]) * jnp.einsum('nhld,nhsd->nhls', qc, kc)
        num = w_state[..., None] * jnp.einsum('nhld,nhde->nhle', qc, C) + jnp.einsum('nhls,nhse->nhle', w_intra, vc)
        den = w_state * jnp.einsum('nhld,nhd->nhl', qc, n) + jnp.sum(w_intra, axis=-1)
        h = num / jnp.maximum(jnp.abs(den), jnp.exp(-m_loc))[..., None]
        g_end = b[..., -1:] - b + igc
        a_end = b[..., -1] + m
        m_new = jnp.maximum(a_end, jnp.max(g_end, axis=-1))
        decay = jnp.exp(a_end - m_new)
        w_end = jnp.exp(g_end - m_new[..., None])
        C_new = decay[..., None, None] * C + jnp.einsum('nhl,nhld,nhle->nhde', w_end, kc, vc)
        n_new = decay[..., None] * n + jnp.einsum('nhl,nhld->nhd', w_end, kc)
        return (C_new, n_new, m_new), h

    init = (jnp.zeros((N, H, Dh, Dh), jnp.float32), jnp.zeros((N, H, Dh), jnp.float32), jnp.zeros((N, H), jnp.float32))
    _, h = lax.scan(step, init, (chunked(q), chunked(k), chunked(v), chunked(ig), chunked(lf)))
    return jnp.moveaxis(h, 0, 2).reshape(N, H, S, Dh)


def mlstm_mixer(qk, v, o_pre, gates, conv_w, conv_b, ig_b, fg_b, norm_g):
    B, S, _ = v.shape
    qk = jax.nn.silu(dwconv3(qk, conv_w, conv_b))
    to_heads = lambda t: t.astype(jnp.float32).reshape(B, S, B_HEADS, HEAD_DIM).transpose(0, 2, 1, 3)
    q = to_heads(qk[..., :B_WIDTH])
    k = to_heads(qk[..., B_WIDTH:]) * HEAD_DIM ** -0.5
    vh = to_heads(v)
    g = gates.astype(jnp.float32).reshape(B, S, 2, 2, B_HEADS)
    ig = (g[:, :, :, 0] + ig_b.astype(jnp.float32)).transpose(2, 0, 3, 1)
    lf = jax.nn.log_sigmoid(g[:, :, :, 1] + fg_b.astype(jnp.float32)).transpose(2, 0, 3, 1)
    flip = lambda t: jnp.flip(t, axis=2)
    both = lambda t: jnp.concatenate([t, flip(t)], axis=0)
    hcat = mlstm_chunkwise(both(q), both(k), both(vh),
                           jnp.concatenate([ig[0], flip(ig[1])], axis=0),
                           jnp.concatenate([lf[0], flip(lf[1])], axis=0))
    hs = (hcat[:B] + flip(hcat[B:])).transpose(0, 2, 1, 3)
    hs = hs * lax.rsqrt(jnp.mean(hs * hs, axis=-1, keepdims=True) + RMS_EPS)
    hs = hs * norm_g.astype(jnp.float32).reshape(B_HEADS, HEAD_DIM)
    return (jax.nn.sigmoid(o_pre.astype(jnp.float32)) * hs.reshape(B, S, B_WIDTH)).astype(v.dtype)


def blocked_gqa(q, k, v):
    B, S, _, Dh = q.shape
    G = C_HEADS // C_KV_HEADS
    nb = S // Q_BLOCK
    qb = (q * Dh ** -0.5).reshape(B, nb, Q_BLOCK, C_KV_HEADS, G, Dh).transpose(1, 0, 2, 3, 4, 5)

    def attend(qblk):
        s = jnp.einsum('bqkgd,bskd->bkgqs', qblk, k).astype(jnp.float32)
        p = jax.nn.softmax(s, axis=-1).astype(v.dtype)
        return jnp.einsum('bkgqs,bskd->bqkgd', p, v)

    o = lax.map(attend, qb)
    return o.transpose(1, 0, 2, 3, 4, 5).reshape(B, S, C_WIDTH)


def parallel_mixers(h, w_in, conv_w, conv_b, ig_b, fg_b, m_norm_g, qk_g, w_out, rope_p, rope_row, rope_col):
    B, S, _ = h.shape
    z = h @ w_in
    idx = np.cumsum(IN_SPLITS)[:-1].tolist()
    a_q, a_k, a_v, b_qk, b_v, b_o, b_g, c_q, c_k, c_v = jnp.split(z, idx, axis=-1)
    heads = lambda t, n: t.reshape(B, S, n, HEAD_DIM)
    qa = partial_rope(heads(a_q, A_HEADS) * HEAD_DIM ** -0.5, rope_p)
    ka = partial_rope(heads(a_k, A_HEADS), rope_p)
    ya = dilated_attention(qa, ka, heads(a_v, A_HEADS)).reshape(B, S, A_WIDTH)
    yb = mlstm_mixer(b_qk, b_v, b_o, b_g, conv_w, conv_b, ig_b, fg_b, m_norm_g)
    qc = axial_rope(rmsnorm(heads(c_q, C_HEADS), qk_g[0]), rope_row, rope_col)
    kc = axial_rope(rmsnorm(heads(c_k, C_KV_HEADS), qk_g[1]), rope_row, rope_col)
    yc = blocked_gqa(qc, kc, heads(c_v, C_KV_HEADS))
    return jnp.concatenate([ya, yb, yc], axis=-1) @ w_out


def memory_cross_attention(h, mem, mem_g, w_q, w_kv, w_o):
    B, S, _ = h.shape
    M = mem.shape[1]
    q = (h @ w_q).reshape(B, S, X_HEADS, X_HEAD_DIM) * X_HEAD_DIM ** -0.5
    kv = rmsnorm(mem, mem_g) @ w_kv
    k = kv[..., :X_WIDTH].reshape(B, M, X_HEADS, X_HEAD_DIM)
    v = kv[..., X_WIDTH:].reshape(B, M, X_HEADS, X_HEAD_DIM)
    s = jnp.einsum('bshd,bmhd->bhsm', q, k).astype(jnp.float32)
    p = jax.nn.softmax(s, axis=-1).astype(v.dtype)
    o = jnp.einsum('bhsm,bmhd->bshd', p, v).reshape(B, S, X_WIDTH)
    return o @ w_o


def conv_ffn(h, w_up, conv_w, conv_b, w_down):
    u = h @ w_up
    gate, val = u[..., :D_FF], u[..., D_FF:]
    gate = dwconv3(gate, conv_w, conv_b)
    return (jax.nn.silu(gate) * val) @ w_down


def encoder_trunk(x, mem, norm_mix_g, w_in, mlstm_conv_w, mlstm_conv_b, mlstm_igate_b, mlstm_fgate_b, mlstm_norm_g,
                  qk_norm_g, w_out, norm_x_g, norm_mem_g, w_xq, w_xkv, w_xo, norm_ffn_g, w_ffn_up, ffn_conv_w,
                  ffn_conv_b, w_ffn_down, final_norm_g):
    B, S, _ = x.shape
    rows = S // GRID_W
    rope_p = rope_cos_sin(jnp.arange(S), ROPE_DIMS, ROPE_THETA)
    row_idx = jnp.repeat(jnp.arange(rows), GRID_W)
    col_idx = jnp.tile(jnp.arange(GRID_W), rows)
    rope_row = rope_cos_sin(row_idx, HEAD_DIM // 2, AXIAL_THETA)
    rope_col = rope_cos_sin(col_idx, HEAD_DIM // 2, AXIAL_THETA)
    for l in range(DEPTH):
        x = x + parallel_mixers(rmsnorm(x, norm_mix_g[l]), w_in[l], mlstm_conv_w[l], mlstm_conv_b[l],
                                mlstm_igate_b[l], mlstm_fgate_b[l], mlstm_norm_g[l], qk_norm_g[l], w_out[l],
                                rope_p, rope_row, rope_col)
        x = x + memory_cross_attention(rmsnorm(x, norm_x_g[l]), mem, norm_mem_g[l], w_xq[l], w_xkv[l], w_xo[l])
        x = x + conv_ffn(rmsnorm(x, norm_ffn_g[l]), w_ffn_up[l], ffn_conv_w[l], ffn_conv_b[l], w_ffn_down[l])
    return rmsnorm(x, final_norm_g)


def setup_inputs(seed: int = 0) -> dict:
    key = jax.random.key(seed)
    ks = jax.random.split(key, 25)
    nrm = lambda k, shape, scale: jax.random.normal(k, shape, jnp.float32) * scale
    gain = lambda k, shape: 1.0 + 0.1 * jax.random.normal(k, shape, jnp.float32)
    fgate = jnp.linspace(3.0, 6.0, B_HEADS)[None, None, :] + nrm(ks[10], (DEPTH, 2, B_HEADS), 0.1)
    return {
        "x_prompt": nrm(ks[0], (BATCH, SEQ, D_MODEL), 1.0),
        "x_sample": nrm(ks[1], (DEC_BATCH, DEC_SEQ, D_MODEL), 1.0),
        "mem_prompt": nrm(ks[2], (BATCH, N_MEM, D_MODEL), 1.0),
        "mem_sample": nrm(ks[3], (DEC_BATCH, N_MEM, D_MODEL), 1.0),
        "norm_mix_g": gain(ks[4], (DEPTH, D_MODEL)),
        "w_in": nrm(ks[5], (DEPTH, D_MODEL, IN_WIDTH), D_MODEL ** -0.5),
        "mlstm_conv_w": nrm(ks[6], (DEPTH, CONV_W, 2 * B_WIDTH), CONV_W ** -0.5),
        "mlstm_conv_b": nrm(ks[7], (DEPTH, 2 * B_WIDTH), 0.02),
        "mlstm_igate_b": nrm(ks[8], (DEPTH, 2, B_HEADS), 0.1),
        "mlstm_fgate_b": fgate,
        "mlstm_norm_g": gain(ks[9], (DEPTH, B_WIDTH)),
        "qk_norm_g": gain(ks[11], (DEPTH, 2, HEAD_DIM)),
        "w_out": nrm(ks[12], (DEPTH, D_MODEL, D_MODEL), D_MODEL ** -0.5),
        "norm_x_g": gain(ks[13], (DEPTH, D_MODEL)),
        "norm_mem_g": gain(ks[14], (DEPTH, D_MODEL)),
        "w_xq": nrm(ks[15], (DEPTH, D_MODEL, X_WIDTH), D_MODEL ** -0.5),
        "w_xkv": nrm(ks[16], (DEPTH, D_MODEL, 2 * X_WIDTH), D_MODEL ** -0.5),
        "w_xo": nrm(ks[17], (DEPTH, X_WIDTH, D_MODEL), X_WIDTH ** -0.5),
        "norm_ffn_g": gain(ks[18], (DEPTH, D_MODEL)),
        "w_ffn_up": nrm(ks[19], (DEPTH, D_MODEL, 2 * D_FF), D_MODEL ** -0.5),
        "ffn_conv_w": nrm(ks[20], (DEPTH, CONV_W, D_FF), CONV_W ** -0.5),
        "ffn_conv_b": nrm(ks[21], (DEPTH, D_FF), 0.02),
        "w_ffn_down": nrm(ks[22], (DEPTH, D_FF, D_MODEL), D_FF ** -0.5),
        "final_norm_g": gain(ks[23], (D_MODEL,)),
    }


def reference(x_prompt, x_sample, mem_prompt, mem_sample, norm_mix_g, w_in, mlstm_conv_w, mlstm_conv_b,
              mlstm_igate_b, mlstm_fgate_b, mlstm_norm_g, qk_norm_g, w_out, norm_x_g, norm_mem_g, w_xq, w_xkv,
              w_xo, norm_ffn_g, w_ffn_up, ffn_conv_w, ffn_conv_b, w_ffn_down, final_norm_g):
    y_prompt = encoder_trunk(x_prompt, mem_prompt, norm_mix_g, w_in, mlstm_conv_w, mlstm_conv_b, mlstm_igate_b,
                             mlstm_fgate_b, mlstm_norm_g, qk_norm_g, w_out, norm_x_g, norm_mem_g, w_xq, w_xkv, w_xo,
                             norm_ffn_g, w_ffn_up, ffn_conv_w, ffn_conv_b, w_ffn_down, final_norm_g)
    y_sample = encoder_trunk(x_sample, mem_sample, norm_mix_g, w_in, mlstm_conv_w, mlstm_conv_b, mlstm_igate_b,
                             mlstm_fgate_b, mlstm_norm_g, qk_norm_g, w_out, norm_x_g, norm_mem_g, w_xq, w_xkv, w_xo,
                             norm_ffn_g, w_ffn_up, ffn_conv_w, ffn_conv_b, w_ffn_down, final_norm_g)
    return (y_prompt, y_sample)
```

```python
import numpy as np
import ml_dtypes
import concourse.bass as bass
import concourse.mybir as mybir
from concourse.bass_utils import run_bass_kernel_spmd

F32 = mybir.dt.float32
BF = mybir.dt.bfloat16
ALU = mybir.AluOpType
AF = mybir.ActivationFunctionType
AX = mybir.AxisListType

D = 1024
INW = 2576
DFF = 2752
NMEM = 256
DEPTH = 2
EPS = 1e-6
NEG = -30000.0
NRING = 24
SAME_ENG_SYNC = True
SB_LO = 16512
SB_HI = 229344


class Tk:
    __slots__ = ("w", "rs", "name")

    def __init__(self, name=""):
        self.w = None
        self.rs = {}
        self.name = name


class Prog:
    CE = ("pe", "act", "dve", "pool")
    ALLE = ("pe", "act", "dve", "pool", "sp")

    def __init__(self, nc):
        self.nc = nc
        self.sem = {}
        for e in self.CE:
            self.sem[("c", e)] = nc.alloc_semaphore("sc_" + e)
        self.cnt = {e: 0 for e in self.CE}
        self.queues = ("sp", "pool")
        self.ring_i = {q: 0 for q in self.queues}
        for q in self.queues:
            for i in range(NRING):
                self.sem[("r", q, i)] = nc.alloc_semaphore("sr_%s_%d" % (q, i))
        self.ring_cnt = {}
        self.waited = {e: {} for e in self.ALLE}
        self.prog = {e: [] for e in self.ALLE}
        self.n_ins = 0
        self._cap = None

    def _deps(self, eng, r, w, extra=()):
        need = {}

        def add(ev):
            if ev is None:
                return
            k, v = ev
            if need.get(k, 0) < v:
                need[k] = v
        own = None
        for t in r:
            add(t.w)
        for t in w:
            if t.w is not None and t.w[0] != own:
                add(t.w)
            for k, v in t.rs.items():
                if k != own:
                    add((k, v))
        for ev in extra:
            add(ev)
        out = []
        wd = self.waited[eng]
        for k, v in need.items():
            if k == ("c", eng):
                if eng == "pe" or not SAME_ENG_SYNC:
                    continue
            if wd.get(k, 0) >= v:
                continue
            wd[k] = v
            out.append((k, v))
        return out

    def _mark(self, ev, r, w):
        k, v = ev
        for t in r:
            if t.rs.get(k, 0) < v:
                t.rs[k] = v
        for t in w:
            t.w = ev
            t.rs = {}

    def capture(self, fn):
        self._cap = []
        fn()
        lst, self._cap = self._cap, None
        return lst

    def emit_interleaved(self, lists):
        idx = [0] * len(lists)
        left = sum(len(x) for x in lists)
        while left:
            for k, lst in enumerate(lists):
                if idx[k] < len(lst):
                    it = lst[idx[k]]
                    idx[k] += 1
                    left -= 1
                    if it[0] == "op":
                        self.op(it[1], it[2], it[3], it[4])
                    else:
                        self.dma(it[1], it[2], it[3], it[4], it[5], **it[6])

    def op(self, eng, fn, r=(), w=()):
        if self._cap is not None:
            self._cap.append(("op", eng, fn, tuple(r), tuple(w)))
            return None
        waits = self._deps(eng, r, w)
        self.cnt[eng] += 1
        k = ("c", eng)
        ev = (k, self.cnt[eng])
        self.prog[eng].append((waits, fn, (k, 1)))
        self._mark(ev, r, w)
        self.n_ins += 1
        return ev

    def dma(self, q, out, in_, r=(), w=(), **kw):
        if self._cap is not None:
            self._cap.append(("dma", q, out, in_, tuple(r), tuple(w), kw))
            return None
        i = self.ring_i[q]
        self.ring_i[q] = (i + 1) % NRING
        k = ("r", q, i)
        c = self.ring_cnt.get(k, 0)
        extra = [(k, 16 * c)] if c > 0 else []
        waits = self._deps(q, r, w, extra)
        self.ring_cnt[k] = c + 1
        ev = (k, 16 * (c + 1))

        def fn(e, out=out, in_=in_, kw=kw):
            return e.dma_start(out=out, in_=in_, **kw)
        self.prog[q].append((waits, fn, (k, 16)))
        self._mark(ev, r, w)
        self.n_ins += 1
        return ev

    def barrier(self):
        evs = [(("c", e), self.cnt[e]) for e in self.CE if self.cnt[e] > 0]
        evs += [(k, 16 * c) for k, c in self.ring_cnt.items()]
        for eng in self.ALLE:
            waits = []
            wd = self.waited[eng]
            for k, v in evs:
                if wd.get(k, 0) >= v:
                    continue
                wd[k] = v
                waits.append((k, v))
            if waits:
                self.prog[eng].append((waits, None, None))

    def replay(self):
        nc = self.nc
        sem = self.sem
        prog = self.prog

        def run(name, e):
            for waits, fn, inc in prog[name]:
                for k, v in waits:
                    e.wait_ge(sem[k], v)
                if fn is not None:
                    ins = fn(e)
                    ins.then_inc(sem[inc[0]], inc[1])
        with nc.Block() as block:
            @block.tensor
            def _(e):
                run("pe", e)

            @block.scalar
            def _(e):
                run("act", e)

            @block.vector
            def _(e):
                run("dve", e)

            @block.gpsimd
            def _(e):
                run("pool", e)

            @block.sync
            def _(e):
                run("sp", e)


class SBAlloc:
    def __init__(self, nc):
        self.nc = nc
        self.off = SB_LO
        self.hi = SB_HI
        self.n = 0

    def alloc_top(self, shape, dt, name="t"):
        sz = 1
        for s in shape[1:]:
            sz *= s
        sz *= 2 if dt == BF else 4
        off = (self.hi - sz) // 64 * 64
        assert off >= self.off, ("SBUF overflow (top)", name, off, sz)
        self.hi = off
        self.n += 1
        return self.nc.alloc_sbuf_tensor_at("%s_%d" % (name, self.n), list(shape), dt, offset=off)

    def reset_top(self):
        self.hi = SB_HI

    def alloc(self, shape, dt, name="t"):
        sz = 1
        for s in shape[1:]:
            sz *= s
        sz *= 2 if dt == BF else 4
        off = (self.off + 63) // 64 * 64
        assert off + sz <= self.hi, ("SBUF overflow", name, off, sz)
        self.off = off + sz
        self.n += 1
        return self.nc.alloc_sbuf_tensor_at("%s_%d" % (name, self.n), list(shape), dt, offset=off)

    def mark(self):
        return self.off

    def release(self, m):
        self.off = m


def ssl(start, count, step):
    return slice(start, start + (count - 1) * step + 1, step)


class Ring:
    def __init__(self, items):
        self.items = items
        self.i = 0

    def next(self):
        it = self.items[self.i % len(self.items)]
        self.i += 1
        return it


def host_consts(smax):
    c = {}
    c["c_ident"] = np.eye(128, dtype=np.float32).astype(ml_dtypes.bfloat16)
    i = np.arange(128)[:, None]
    j = np.arange(128)[None, :]
    c["c_mf"] = (i <= j).astype(np.float32)
    c["c_mb"] = (i >= j).astype(np.float32)
    ma = np.where(i >= j, 0.0, NEG).astype(np.float32)
    mb = np.where(i <= j, 0.0, NEG).astype(np.float32)
    m1 = np.zeros((128, 128), np.float32)
    m1[:64] = ma[64:]
    a01 = (i >= j).astype(np.float32)
    b01 = (i <= j).astype(np.float32)
    f01 = np.zeros((128, 128), np.float32)
    f01[:64] = a01[64:]
    c["c_ma"] = np.concatenate([a01, b01], 1).astype(ml_dtypes.bfloat16)
    c["c_m1"] = np.concatenate([f01, b01], 1).astype(ml_dtypes.bfloat16)
    pos = np.arange(smax, dtype=np.float32)
    inv = (np.float32(500000.0) ** (-np.arange(0, 16, 2, dtype=np.float32) / np.float32(16))).astype(np.float32)
    ang = (pos[:, None] * inv[None, :]).astype(np.float32)
    c["c_ropeA"] = np.concatenate([np.cos(ang), np.sin(ang)], 1).astype(np.float32)
    inv2 = (np.float32(10000.0) ** (-np.arange(0, 32, 2, dtype=np.float32) / np.float32(32))).astype(np.float32)
    row = (np.arange(smax) // 64).astype(np.float32)
    col = (np.arange(smax) % 64).astype(np.float32)
    ar = (row[:, None] * inv2[None, :]).astype(np.float32)
    ac = (col[:, None] * inv2[None, :]).astype(np.float32)
    c["c_ropeC"] = np.concatenate([np.cos(ar), np.cos(ac), np.sin(ar), np.sin(ac)], 1).astype(np.float32)
    return c


WSHAPES = {
    "norm_mix_g": (DEPTH, D), "w_in": (DEPTH, D, INW), "mlstm_conv_w": (DEPTH, 3, 512), "mlstm_conv_b": (DEPTH, 512),
    "mlstm_igate_b": (DEPTH, 2, 4), "mlstm_fgate_b": (DEPTH, 2, 4), "mlstm_norm_g": (DEPTH, 256),
    "qk_norm_g": (DEPTH, 2, 64), "w_out": (DEPTH, D, D), "norm_x_g": (DEPTH, D), "norm_mem_g": (DEPTH, D),
    "w_xq": (DEPTH, D, 256), "w_xkv": (DEPTH, D, 512), "w_xo": (DEPTH, 256, D), "norm_ffn_g": (DEPTH, D),
    "w_ffn_up": (DEPTH, D, 2 * DFF), "ffn_conv_w": (DEPTH, 3, DFF), "ffn_conv_b": (DEPTH, DFF),
    "w_ffn_down": (DEPTH, DFF, D), "final_norm_g": (D,),
}
CSHAPES = {"c_ident": ((128, 128), BF), "c_mf": ((128, 128), F32), "c_mb": ((128, 128), F32),
           "c_ma": ((128, 256), BF), "c_m1": ((128, 256), BF),
           "c_ropeA": ((4096, 16), F32), "c_ropeC": ((4096, 64), F32)}


def build_program(seq_lens, depth=DEPTH, dbg=False, phases=("mix", "mixout", "cross", "ffn"), groups=("C", "A", "B")):
    nc = bass.Bass("TRN2", target_bir_lowering=False)
    P = Prog(nc)
    sb = SBAlloc(nc)
    NS = len(seq_lens)
    TOT = sum(seq_lens)
    SOFF = [sum(seq_lens[:i]) for i in range(NS)]
    SMAX = max(seq_lens)

    def din(name, shape, dt=F32):
        return nc.dram_tensor(name, list(shape), dt, kind="ExternalInput").ap()

    xs = [din("xs%d" % i, (S, D)) for i, S in enumerate(seq_lens)]
    msrc = [din("ms%d" % i, (NMEM, D)) for i in range(NS)]
    ys = [nc.dram_tensor("ys%d" % i, [S, D], F32, kind="ExternalOutput").ap() for i, S in enumerate(seq_lens)]
    Wd = {n: din(n, s) for n, s in WSHAPES.items()}
    Cd = {n: din(n, s, dt) for n, (s, dt) in CSHAPES.items()}
    skind = "ExternalOutput" if dbg else "Internal"
    XA = nc.dram_tensor("XA", [TOT, D], F32, kind=skind).ap()
    XB = nc.dram_tensor("XB", [TOT, D], F32, kind=skind).ap()
    XC = nc.dram_tensor("XC", [TOT, D], F32, kind=skind).ap()
    YT = nc.dram_tensor("YT", [D, TOT], BF, kind=skind).ap()
    ZAV = nc.dram_tensor("ZAV", [TOT, 256], BF, kind="Internal").ap()
    XAt = [Tk() for _ in range(TOT // 128)]
    XBt = [Tk() for _ in range(TOT // 128)]
    XCt = [Tk() for _ in range(TOT // 128)]
    YTt = [[Tk() for _ in range(TOT // 512)] for _ in range(16)]
    ZAVt = [Tk() for _ in range(TOT // 128)]
    NOTK = []

    PD = [nc.alloc_psum_tensor("pd%d" % i, [128, 2, 512], F32) for i in range(4)]
    PBK = [Tk("bank%d" % i) for i in range(8)]

    def bank(i):
        return PD[i // 2][:, i % 2, :]

    def bank_bf(i):
        return PD[i // 2][:].bitcast(BF)[:, i % 2, :]

    def cload(name, shape, dt, src, q="sp"):
        t = sb.alloc(shape, dt, name)
        tk = Tk(name)
        P.dma(q, t[:], src, w=[tk])
        return t, tk

    ident, identk = cload("ident", [128, 128], BF, Cd["c_ident"])
    mf32, mf32k = cload("mf32", [128, 128], F32, Cd["c_mf"])
    mb32, mb32k = cload("mb32", [128, 128], F32, Cd["c_mb"])
    maA, maAk = cload("maA", [128, 2, 128], BF, Cd["c_ma"].rearrange("p (a q) -> p a q", a=2))
    m1A, m1Ak = cload("m1A", [128, 2, 128], BF, Cd["c_m1"].rearrange("p (a q) -> p a q", a=2))
    hmask = sb.alloc([128, 4, 65], F32, "hmask")
    hmaskk = Tk()
    P.op("pool", lambda e: e.memset(hmask[:], 0.0), w=[hmaskk])
    P.op("pool", lambda e: e.memset(hmask[0:64].rearrange("p (a e) c -> p a e c", e=2)[:, :, 0, :], 1.0), w=[hmaskk])
    P.op("pool", lambda e: e.memset(hmask[64:128].rearrange("p (a e) c -> p a e c", e=2)[:, :, 1, :], 1.0), w=[hmaskk])
    NTMAX = SMAX // 128
    ones32 = sb.alloc([128, 128], F32, "ones32")
    ones32k = Tk()
    P.op("dve", lambda e: e.memset(ones32[:], 1.0), w=[ones32k])
    cst = sb.alloc([128, 8], F32, "cst")
    cstk = Tk()
    P.op("dve", lambda e: e.memset(cst[:, 0:1], EPS), w=[cstk])
    P.op("dve", lambda e: e.memset(cst[:, 1:2], -0.5), w=[cstk])
    P.op("dve", lambda e: e.memset(cst[:, 2:3], float(np.log(0.125))), w=[cstk])
    P.op("dve", lambda e: e.memset(cst[:, 3:4], 0.0), w=[cstk])

    stat = Ring([(sb.alloc([128, 32], F32, "stat"), Tk()) for _ in range(4)])
    xring = Ring([(sb.alloc([128, D], F32, "xt"), Tk()) for _ in range(3)])
    junk = sb.alloc([128, D], BF, "junk")
    junkk = Tk()
    hring = Ring([(sb.alloc([128, D], BF, "hb"), Tk()) for _ in range(2)])
    gbuf_box = [None]

    def interleaved(n, body, width=2):
        for i0 in range(0, n, width):
            lists = [P.capture(lambda i=i: body(i)) for i in range(i0, min(n, i0 + width))]
            P.emit_interleaved(lists)

    def load_gain(vec_ap):
        g, gk = gbuf_box[0].next()
        P.dma("sp", g[:], vec_ap.partition_broadcast(128), w=[gk])
        return g, gk

    def rstd_from_ss(st, stk, ncol, inv_n):
        P.op("dve", lambda e: e.tensor_scalar(out=st[:, 2 * ncol:3 * ncol], in0=st[:, 0:ncol], scalar1=inv_n, scalar2=EPS,
                                               op0=ALU.mult, op1=ALU.add), r=[stk], w=[stk])
        P.op("pool", lambda e: e.tensor_tensor(out=st[:, ncol:2 * ncol], in0=st[:, 2 * ncol:3 * ncol],
                                                in1=cst[:, 1:2].broadcast_to([128, ncol]), op=ALU.pow),
             r=[stk, cstk], w=[stk])

    def norm_rows(xt, xtk, g, gk, rows=128):
        st, stk = stat.next()
        P.op("dve", lambda e: e.scalar_tensor_tensor(out=junk[0:rows, :], in0=xt[0:rows, :], scalar=1.0, in1=xt[0:rows, :],
                                                      op0=ALU.mult, op1=ALU.mult, accum_out=st[0:rows, 0:1]),
             r=[xtk], w=[junkk, stk])
        rstd_from_ss(st, stk, 1, 1.0 / D)
        h, hk = hring.next()
        P.op("dve", lambda e: e.scalar_tensor_tensor(out=h[0:rows, :], in0=xt[0:rows, :], scalar=st[0:rows, 1:2], in1=g[0:rows, :],
                                                      op0=ALU.mult, op1=ALU.mult), r=[xtk, stk, gk], w=[hk])
        return h, hk

    tb = [6, 7]
    tbi = [0]

    def transpose_to(h, hk, nblk, dst_fn, dstk, rows=128):
        b = tb[tbi[0] % 2]
        tbi[0] += 1
        pv = bank_bf(b)[:, 0:nblk * 128].rearrange("p (k t) -> p k t", k=nblk)

        def fn(e):
            for k in range(nblk):
                ins = e.transpose(out=pv[:, k, 0:rows], in_=h[0:rows, k * 128:(k + 1) * 128], identity=ident[0:rows, 0:rows])
            return ins
        P.op("pe", fn, r=[hk, identk], w=[PBK[b]])
        P.op("act", lambda e: e.copy(out=dst_fn(), in_=pv[:, :, 0:rows]), r=[PBK[b]], w=dstk)

    def build_hT(src, srctk, row0, S, gvec, hT, hTk):
        g, gk = load_gain(gvec)
        P.op("pool", lambda e: e.memset(hT[:, :, 0:1], 0.0), w=[hTk[0]])
        P.op("pool", lambda e: e.memset(hT[:, :, S + 1:S + 2], 0.0), w=[hTk[-1]])
        def body(i):
            xt, xtk = xring.next()
            P.dma("sp", xt[:], src[row0 + i * 128: row0 + (i + 1) * 128, :], r=[srctk[(row0 // 128) + i]] if srctk else [], w=[xtk])
            h, hk = norm_rows(xt, xtk, g, gk)
            transpose_to(h, hk, 8, lambda i=i: hT[:, :, 1 + i * 128: 1 + (i + 1) * 128], [hTk[i]])
        interleaved(S // 128, body)

    NRS = 8
    RS = nc.dram_tensor("RS", [NRS, 2, 512], F32, kind="Internal").ap()
    RSk = [Tk() for _ in range(NRS)]
    rs_i = [0]

    def norm_OT_a(ob, nq):
        osb, osbk = aux["osb"].next()
        P.op("act", lambda e: e.copy(out=osb[0:65, 0:nq], in_=bank(ob)[0:65, 0:nq]), r=[PBK[ob]], w=[osbk])
        P.op("dve", lambda e: e.reciprocal(out=osb[64:65, 0:nq], in_=osb[64:65, 0:nq]), r=[osbk], w=[osbk])
        j = rs_i[0] % NRS
        rs_i[0] += 1
        P.dma("sp", RS[j, 0:1, 0:nq], osb[64:65, 0:nq], r=[osbk], w=[RSk[j]])
        rc, rck = aux["recb"].next()
        P.dma("sp", rc[0:64, 0:nq], RS[j, 0, 0:nq].partition_broadcast(64), r=[RSk[j]], w=[rck])
        return osb, osbk, rc, rck

    def norm_OT_b(st, nq, dst, dstk):
        osb, osbk, rc, rck = st
        P.op("dve", lambda e: e.tensor_tensor(out=dst, in0=osb[0:64, 0:nq], in1=rc[0:64, 0:nq], op=ALU.mult),
             r=[osbk, rck], w=dstk)

    aux = {}

    def alloc_aux(ngain, need_norm=True):
        gbuf_box[0] = Ring([(sb.alloc([128, D], F32, "gb"), Tk()) for _ in range(ngain)])
        if need_norm:
            aux["osb"] = Ring([(sb.alloc([65, 512], F32, "osb"), Tk()) for _ in range(3)])
            aux["recb"] = Ring([(sb.alloc([64, 512], F32, "recb"), Tk()) for _ in range(3)])
            aux["dent"] = Ring([(sb.alloc([128, 8], F32, "dent"), Tk()) for _ in range(4)])

    def xbuf_of(l, stage):
        if stage == 0:
            if l == 0:
                return None
            return XC, XCt
        return (XA, XAt) if stage == 1 else (XB, XBt)

    def mixers(l, si):
        S = seq_lens[si]
        NT = S // 128
        r0 = SOFF[si]
        t0g = r0 // 128
        c0g = r0 // 512
        m0 = sb.mark()
        alloc_aux(1)
        ropeA, ropeAk = cload("ropeA", [128, NT, 16], F32, Cd["c_ropeA"][0:S].rearrange("(t p) c -> p t c", p=128))
        ropeC, ropeCk = cload("ropeC", [128, NT, 64], F32, Cd["c_ropeC"][0:S].rearrange("(t p) c -> p t c", p=128))
        if l == 0:
            src, srctk, srow0 = xs[si], None, 0
        else:
            src, srctk, srow0 = XC, XCt, r0
        mh = sb.mark()
        hT = sb.alloc([128, 8, S + 2], BF, "hT")
        hTk = [Tk() for _ in range(NT)]
        build_hT(src, srctk, srow0, S, Wd["norm_mix_g"][l], hT, hTk)

        def new_wg():
            return sb.alloc([128, 8, 1040], BF, "WG"), Tk()

        def load_wg(WG, WGk, c0, ncol):
            P.dma("pool", WG[:, :, 0:ncol], Wd["w_in"][l][:, c0:c0 + ncol].rearrange("(k p) n -> p k n", p=128), w=[WGk])

        def proj_tok(WG, i, c0, ncol, b, off=0):
            def fn(e):
                for k in range(8):
                    ins = e.matmul(bank(b)[:, off:off + ncol], lhsT=hT[:, k, 1 + i * 128:1 + (i + 1) * 128],
                                   rhs=WG[:, k, c0:c0 + ncol], start=(k == 0), stop=(k == 7))
                return ins
            return fn

        m1 = sb.mark()
        if "C" in groups:
            CQK = sb.alloc([128, 6, S], BF, "CQK")
            CQKk = [Tk() for _ in range(NT)]
            CV = sb.alloc([128, NT, 2, 65], BF, "CV")
            CVk = [Tk() for _ in range(NT)]
            mC = sb.mark()
            WG, WGk = new_wg()
            load_wg(WG, WGk, 1808, 768)
            gtab = sb.alloc([128, 640], F32, "gtab")
            gtabk = Tk()
            gq = Wd["qk_norm_g"][l]
            gtmp = sb.alloc([128, 128], F32, "gtmp")
            gtmpk = Tk()
            P.dma("sp", gtmp[:], gq.rearrange("a d -> (a d)").partition_broadcast(128), w=[gtmpk])
            P.op("dve", lambda e: e.tensor_scalar(out=gtab[:, 0:512].rearrange("p (h d) -> p h d", h=8),
                                                   in0=gtmp[:, 0:64].unsqueeze(1).broadcast_to([128, 8, 64]),
                                                   scalar1=0.125, scalar2=None, op0=ALU.mult), r=[gtmpk], w=[gtabk])
            P.op("dve", lambda e: e.tensor_copy(out=gtab[:, 512:640].rearrange("p (h d) -> p h d", h=2),
                                                 in_=gtmp[:, 64:128].unsqueeze(1).broadcast_to([128, 2, 64])), r=[gtmpk], w=[gtabk])
            P.op("pool", lambda e: e.memset(CV[:, :, :, 64:65], 1.0), w=CVk)
            zc_r = Ring([(sb.alloc([128, 640], F32, "zc"), Tk()) for _ in range(2)])
            sq_r = Ring([(sb.alloc([128, 640], F32, "sq"), Tk()) for _ in range(2)])
            ra_r = Ring([(sb.alloc([128, 320], F32, "ra"), Tk()) for _ in range(2)])
            rb_r = Ring([(sb.alloc([128, 320], F32, "rb"), Tk()) for _ in range(2)])
            rot_r = Ring([(sb.alloc([128, 768], BF, "rot"), Tk()) for _ in range(2)])
            def bodyC(i):
                b0 = (i % 3) * 2
                P.op("pe", proj_tok(WG, i, 0, 512, b0), r=[hTk[i], WGk], w=[PBK[b0]])
                P.op("pe", proj_tok(WG, i, 512, 256, b0 + 1), r=[hTk[i], WGk], w=[PBK[b0 + 1]])
                zc, zck = zc_r.next()
                P.op("act", lambda e, zc=zc, b0=b0: e.copy(out=zc[:, 0:512], in_=bank(b0)[:, 0:512]), r=[PBK[b0]], w=[zck])
                P.op("act", lambda e, zc=zc, b0=b0: e.copy(out=zc[:, 512:640], in_=bank(b0 + 1)[:, 0:128]), r=[PBK[b0 + 1]], w=[zck])
                P.op("act", lambda e, i=i, b0=b0: e.copy(out=CV[:, i, :, 0:64], in_=bank(b0 + 1)[:, 128:256].rearrange("p (g d) -> p g d", g=2)),
                     r=[PBK[b0 + 1]], w=[CVk[i]])
                sq, sqk = sq_r.next()
                st, stk = stat.next()
                P.op("dve", lambda e, zc=zc, sq=sq: e.tensor_tensor(out=sq[:], in0=zc[:], in1=zc[:], op=ALU.mult), r=[zck], w=[sqk])
                P.op("dve", lambda e, sq=sq, st=st: e.tensor_reduce(out=st[:, 0:10], in_=sq[:].rearrange("p (h d) -> p h d", h=10),
                                                                    axis=AX.X, op=ALU.add), r=[sqk], w=[stk])
                rstd_from_ss(st, stk, 10, 1.0 / 64)
                P.op("dve", lambda e, zc=zc, sq=sq, st=st: e.tensor_tensor(
                    out=sq[:].rearrange("p (h d) -> p h d", h=10), in0=zc[:].rearrange("p (h d) -> p h d", h=10),
                    in1=st[:, 10:20].unsqueeze(2).broadcast_to([128, 10, 64]), op=ALU.mult), r=[zck, stk], w=[sqk])
                P.op("dve", lambda e, zc=zc, sq=sq: e.tensor_tensor(out=zc[:], in0=sq[:], in1=gtab[:], op=ALU.mult), r=[sqk, gtabk], w=[zck])
                zv = zc[:].rearrange("p (h a b c) -> p h a b c", h=10, a=2, b=2)
                x1 = zv[:, :, :, 0, :]
                x2 = zv[:, :, :, 1, :]
                cosT = ropeC[:, i, 0:32].rearrange("p (a c) -> p a c", a=2).unsqueeze(1).broadcast_to([128, 10, 2, 16])
                sinT = ropeC[:, i, 32:64].rearrange("p (a c) -> p a c", a=2).unsqueeze(1).broadcast_to([128, 10, 2, 16])
                ra, rak = ra_r.next()
                rb, rbk = rb_r.next()
                rav = ra[:].rearrange("p (h a c) -> p h a c", h=10, a=2)
                rbv = rb[:].rearrange("p (h a c) -> p h a c", h=10, a=2)
                rot, rotk = rot_r.next()
                rv = rot[:, 0:640].rearrange("p (h a b c) -> p h a b c", h=10, a=2, b=2)
                P.op("dve", lambda e, x1=x1, cosT=cosT, rav=rav: e.tensor_tensor(out=rav, in0=x1, in1=cosT, op=ALU.mult), r=[zck, ropeCk], w=[rak])
                P.op("dve", lambda e, x2=x2, sinT=sinT, rbv=rbv: e.tensor_tensor(out=rbv, in0=x2, in1=sinT, op=ALU.mult), r=[zck, ropeCk], w=[rbk])
                P.op("dve", lambda e, rav=rav, rbv=rbv, rv=rv: e.tensor_tensor(out=rv[:, :, :, 0, :], in0=rav, in1=rbv, op=ALU.subtract), r=[rak, rbk], w=[rotk])
                P.op("dve", lambda e, x1=x1, sinT=sinT, rav=rav: e.tensor_tensor(out=rav, in0=x1, in1=sinT, op=ALU.mult), r=[zck, ropeCk], w=[rak])
                P.op("dve", lambda e, x2=x2, cosT=cosT, rbv=rbv: e.tensor_tensor(out=rbv, in0=x2, in1=cosT, op=ALU.mult), r=[zck, ropeCk], w=[rbk])
                P.op("dve", lambda e, rav=rav, rbv=rbv, rv=rv: e.tensor_tensor(out=rv[:, :, :, 1, :], in0=rav, in1=rbv, op=ALU.add), r=[rak, rbk], w=[rotk])
                P.op("pool", lambda e, rot=rot: e.tensor_copy(out=rot[:, 640:768].rearrange("p (a d) -> p a d", a=2),
                                                               in_=rot[:, 576:640].unsqueeze(1).broadcast_to([128, 2, 64])), r=[rotk], w=[rotk])
                P.op("pool", lambda e, rot=rot: e.tensor_copy(out=rot[:, 576:640], in_=rot[:, 512:576]), r=[rotk], w=[rotk])
                transpose_to(rot, rotk, 6, lambda i=i: CQK[:, :, i * 128:(i + 1) * 128], [CQKk[i]])
            interleaved(NT, bodyC)
            P.barrier()
            sb.release(mC)
            pbuf = Ring([(sb.alloc([128, 2, 512], BF, "pbuf"), Tk()) for _ in range(4)])
            ybuf = Ring([(sb.alloc([64, 512], BF, "ybuf"), Tk()) for _ in range(4)])
            for g in range(2):
                for hp2 in range(2):
                    pi = 2 * g + hp2
                    for qc in range(S // 512):
                        ob = [6, 7]
                        qcols = slice(qc * 512, (qc + 1) * 512)
                        qtk = CQKk[qc * 4:(qc + 1) * 4]

                        def s_mm(kt, g=g, pi=pi, qcols=qcols, qtk=qtk):
                            sj = kt % 3

                            def fn(e):
                                for ee in range(2):
                                    ins = e.matmul(PD[sj][:, ee, :], lhsT=CQK[64 * ee:64 * ee + 64, 4 + g, kt * 128:(kt + 1) * 128],
                                                   rhs=CQK[64 * ee:64 * ee + 64, pi, qcols], start=True, stop=True)
                                return ins
                            P.op("pe", fn, r=[CQKk[kt]] + qtk, w=[PBK[2 * sj], PBK[2 * sj + 1]])

                        def pv_mm(kt, pb_, pbk_, g=g, ob=ob):
                            def fn(e):
                                for ee in range(2):
                                    ins = e.matmul(bank(ob[ee])[0:65, :], lhsT=CV[:, kt, g, :], rhs=pb_[:, ee, :],
                                                   start=(kt == 0), stop=(kt == NT - 1))
                                return ins
                            P.op("pe", fn, r=[CVk[kt], pbk_], w=[PBK[ob[0]], PBK[ob[1]]])

                        s_mm(0)
                        s_mm(1)
                        for kt in range(NT):
                            sj = kt % 3
                            if kt + 2 < NT:
                                s_mm(kt + 2)
                            pb_, pbk_ = pbuf.next()
                            P.op("act", lambda e, pb_=pb_, sj=sj: e.activation(out=pb_[:], in_=PD[sj][:], func=AF.Exp),
                                 r=[PBK[2 * sj], PBK[2 * sj + 1]], w=[pbk_])
                            pv_mm(kt, pb_, pbk_)
                        sts = [norm_OT_a(ob[ee], 512) for ee in range(2)]
                        for ee in range(2):
                            h = 4 * g + 2 * hp2 + ee
                            yb, ybk = ybuf.next()
                            norm_OT_b(sts[ee], 512, yb[:, :], [ybk])
                            rg = 8 + h
                            P.dma("sp", YT[512 + 64 * h: 512 + 64 * h + 64, r0 + qc * 512: r0 + (qc + 1) * 512], yb[:, :],
                                  r=[ybk], w=[YTt[rg][c0g + qc]])
            P.barrier()
            sb.release(m1)

        if "A" in groups:
            AQK = sb.alloc([128, 4, S], BF, "AQK")
            AQKk = Tk()
            mA = sb.mark()
            WG, WGk = new_wg()
            load_wg(WG, WGk, 0, 768)
            zc_r = Ring([(sb.alloc([128, 512], F32, "zcA"), Tk()) for _ in range(2)])
            ta_r = Ring([(sb.alloc([128, 4, 64], F32, "taA"), Tk()) for _ in range(2)])
            zb_r = Ring([(sb.alloc([128, 512], BF, "zbA"), Tk()) for _ in range(2)])
            vt_r = Ring([(sb.alloc([128, 256], BF, "vtA"), Tk()) for _ in range(2)])
            def bodyA(i):
                b0 = (i % 3) * 2
                P.op("pe", proj_tok(WG, i, 0, 512, b0), r=[hTk[i], WGk], w=[PBK[b0]])
                P.op("pe", proj_tok(WG, i, 512, 256, b0 + 1), r=[hTk[i], WGk], w=[PBK[b0 + 1]])
                zc, zck = zc_r.next()
                P.op("act", lambda e, zc=zc, b0=b0: e.copy(out=zc[:], in_=bank(b0)[:, :]), r=[PBK[b0]], w=[zck])
                vt, vtk = vt_r.next()
                P.op("act", lambda e, vt=vt, b0=b0: e.copy(out=vt[:], in_=bank(b0 + 1)[:, 0:256]), r=[PBK[b0 + 1]], w=[vtk])
                P.dma("sp", ZAV[r0 + i * 128: r0 + (i + 1) * 128, :], vt[:], r=[vtk], w=[ZAVt[t0g + i]])
                zv = zc[:].rearrange("p (h d) -> p h d", h=8)
                x1 = zv[:, :, 0:8]
                x2 = zv[:, :, 8:16]
                cosT = ropeA[:, i, 0:8].unsqueeze(1).broadcast_to([128, 8, 8])
                sinT = ropeA[:, i, 8:16].unsqueeze(1).broadcast_to([128, 8, 8])
                ta, tak = ta_r.next()
                tv = ta[:].rearrange("p a (h c) -> p a h c", h=8)
                P.op("dve", lambda e, x1=x1, cosT=cosT, tv=tv: e.tensor_tensor(out=tv[:, 0], in0=x1, in1=cosT, op=ALU.mult), r=[zck, ropeAk], w=[tak])
                P.op("dve", lambda e, x2=x2, sinT=sinT, tv=tv: e.tensor_tensor(out=tv[:, 1], in0=x2, in1=sinT, op=ALU.mult), r=[zck, ropeAk], w=[tak])
                P.op("dve", lambda e, x1=x1, sinT=sinT, tv=tv: e.tensor_tensor(out=tv[:, 2], in0=x1, in1=sinT, op=ALU.mult), r=[zck, ropeAk], w=[tak])
                P.op("dve", lambda e, x2=x2, cosT=cosT, tv=tv: e.tensor_tensor(out=tv[:, 3], in0=x2, in1=cosT, op=ALU.mult), r=[zck, ropeAk], w=[tak])
                P.op("dve", lambda e, x1=x1, tv=tv: e.tensor_tensor(out=x1, in0=tv[:, 0], in1=tv[:, 1], op=ALU.subtract), r=[tak], w=[zck])
                P.op("dve", lambda e, x2=x2, tv=tv: e.tensor_tensor(out=x2, in0=tv[:, 2], in1=tv[:, 3], op=ALU.add), r=[tak], w=[zck])
                zb, zbk = zb_r.next()
                P.op("pool", lambda e, zb=zb, zc=zc: e.tensor_copy(out=zb[:], in_=zc[:]), r=[zck], w=[zbk])
                transpose_to(zb, zbk, 4, lambda i=i: AQK[:, :, i * 128:(i + 1) * 128], [AQKk])
            interleaved(NT, bodyA)
            P.barrier()
            sb.release(mA)
            OACC = sb.alloc([65, 2, S], F32, "OACC")
            OACCk = Tk()
            BRS = ((128, 1), (512, 4), (2048, 16))
            NTV = max(d * (S // d // 128 + 1) for (_, d) in BRS)
            VDs = []
            for _vi in range(2):
                VD_ = sb.alloc([128, NTV, 2, 65], BF, "VD")
                VDk_ = Tk()
                P.op("pool", lambda e, VD_=VD_: e.memset(VD_[:, :, :, 64:65], 1.0), w=[VDk_])
                VDs.append((VD_, VDk_))
            pA = Ring([(sb.alloc([128, 2, 2, 128], BF, "pA"), Tk()) for _ in range(3)])
            ybuf = Ring([(sb.alloc([64, 512], BF, "ybufA"), Tk()) for _ in range(2)])
            zsrc = ZAV[r0:r0 + S, :]
            ztk = ZAVt[t0g:t0g + NT]
            units = [(hp, d) for hp in range(2) for (_, d) in BRS]

            def load_unit(u):
                hp, d = units[u]
                VD, VDk = VDs[u % 2]
                L = S // d
                nb = L // 128
                zv = zsrc.rearrange("(j d) c -> d j c", d=d)
                for r in range(d):
                    tb0 = r * (nb + 1)
                    P.dma("sp", VD[0:64, tb0, :, 0:64], zv[r, 0:64, hp * 128:(hp + 1) * 128].rearrange("j (h c) -> j h c", h=2),
                          r=ztk, w=[VDk])
                    P.dma("sp", VD[0:64, tb0 + nb, :, 0:64], zv[r, L - 64:L, hp * 128:(hp + 1) * 128].rearrange("j (h c) -> j h c", h=2),
                          r=ztk, w=[VDk])
                    for m in range(1, nb):
                        P.dma("sp", VD[:, tb0 + m, :, 0:64],
                              zv[r, 128 * m - 64:128 * m + 64, hp * 128:(hp + 1) * 128].rearrange("j (h c) -> j h c", h=2),
                              r=ztk, w=[VDk])

            load_unit(0)
            for u, (hp, d) in enumerate(units):
                if u + 1 < len(units):
                    load_unit(u + 1)
                VD, VDk = VDs[u % 2]
                if u % 3 == 0:
                    P.op("pool", lambda e: e.memset(OACC[:], 0.0), w=[OACCk])
                L = S // d
                nb = L // 128
                blocks = []
                for r in range(d):
                    tb0 = r * (nb + 1)
                    for blk in range(nb):
                        j0 = 128 * blk
                        qsl = ssl(j0 * d + r, 128, d)
                        if blk == 0:
                            Ka, ksa, mka, mkak = 64, ssl(r, 64, d), m1A, m1Ak
                        else:
                            Ka, ksa, mka, mkak = 128, ssl((j0 - 64) * d + r, 128, d), maA, maAk
                        if blk == nb - 1:
                            Kb, ksb = 64, ssl((j0 + 64) * d + r, 64, d)
                        else:
                            Kb, ksb = 128, ssl((j0 + 64) * d + r, 128, d)
                        blocks.append((qsl, Ka, ksa, mka, mkak, Kb, ksb, tb0 + blk, tb0 + blk + 1))
                nblk = len(blocks)

                def emit_S(bi, hp=hp, blocks=blocks):
                    qsl, Ka, ksa, mka, mkak, Kb, ksb, ta_, tb_ = blocks[bi]
                    sj = bi % 3
                    sview = PD[sj][:, :, 0:256].rearrange("p e (a q) -> p e a q", a=2)

                    def fn(e):
                        for ee in range(2):
                            pbs = slice(64 * ee, 64 * ee + 64)
                            e.matmul(sview[0:Ka, ee, 0, :], lhsT=AQK[pbs, 2 + hp, ksa], rhs=AQK[pbs, hp, qsl], start=True, stop=True)
                            ins = e.matmul(sview[0:Kb, ee, 1, :], lhsT=AQK[pbs, 2 + hp, ksb], rhs=AQK[pbs, hp, qsl], start=True, stop=True)
                        return ins
                    P.op("pe", fn, r=[AQKk], w=[PBK[2 * sj], PBK[2 * sj + 1]])

                pas = {}

                def emit_p(bi, blocks=blocks, pas=pas):
                    qsl, Ka, ksa, mka, mkak, Kb, ksb, ta_, tb_ = blocks[bi]
                    sj = bi % 3
                    sview = PD[sj][:, :, 0:256].rearrange("p e (a q) -> p e a q", a=2)
                    sbks = [PBK[2 * sj], PBK[2 * sj + 1]]
                    pa, pak = pA.next()
                    pas[bi] = (pa, pak)
                    P.op("act", lambda e: e.activation(out=pa[:], in_=sview, func=AF.Exp, scale=0.125), r=sbks, w=[pak])
                    P.op("dve", lambda e: e.tensor_tensor(out=pa[:], in0=pa[:], in1=mka[:].unsqueeze(1).broadcast_to([128, 2, 2, 128]),
                                                          op=ALU.mult), r=[pak, mkak], w=[pak])

                def emit_o(bi, blocks=blocks, VD=VD, VDk=VDk, pas=pas):
                    qsl, Ka, ksa, mka, mkak, Kb, ksb, ta_, tb_ = blocks[bi]
                    obk = 6 + bi % 2
                    pa, pak = pas.pop(bi)
                    oview = bank(obk)[:, 0:256].rearrange("p (e q) -> p e q", e=2)

                    def fn2(e):
                        for ee in range(2):
                            e.matmul(oview[0:65, ee, :], lhsT=VD[0:Ka, ta_, ee, :], rhs=pa[0:Ka, ee, 0, :], start=True, stop=False)
                            ins = e.matmul(oview[0:65, ee, :], lhsT=VD[0:Kb, tb_, ee, :], rhs=pa[0:Kb, ee, 1, :], start=False, stop=True)
                        return ins
                    P.op("pe", fn2, r=[VDk, pak], w=[PBK[obk]])
                    P.op("dve", lambda e: e.tensor_tensor(out=OACC[0:65, :, qsl], in0=OACC[0:65, :, qsl], in1=oview[0:65, :, :], op=ALU.add),
                         r=[PBK[obk], OACCk], w=[OACCk])

                emit_S(0)
                if nblk > 1:
                    emit_S(1)
                emit_p(0)
                for bi in range(nblk):
                    if bi + 2 < nblk:
                        emit_S(bi + 2)
                    if bi + 1 < nblk:
                        emit_p(bi + 1)
                    emit_o(bi)
                if u % 3 != 2:
                    continue
                for qc in range(S // 512):
                    for ee in range(2):
                        h = 2 * hp + ee
                        bb = 0
                        P.op("pe", lambda e, ee=ee, qc=qc: e.matmul(bank(0)[0:64, :], lhsT=ones32[64:65, 0:64],
                                                                      rhs=OACC[64:65, ee, qc * 512:(qc + 1) * 512], start=True, stop=True),
                             r=[ones32k, OACCk], w=[PBK[bb]])
                        rc, rck = aux["recb"].next()
                        P.op("dve", lambda e, rc=rc: e.reciprocal(out=rc[0:64, :], in_=bank(0)[0:64, :]), r=[PBK[bb]], w=[rck])
                        yb, ybk = ybuf.next()
                        P.op("dve", lambda e, yb=yb, rc=rc, ee=ee, qc=qc: e.tensor_tensor(out=yb[:, :], in0=OACC[0:64, ee, qc * 512:(qc + 1) * 512],
                                                                                          in1=rc[0:64, :], op=ALU.mult), r=[OACCk, rck], w=[ybk])
                        P.dma("sp", YT[64 * h:64 * h + 64, r0 + qc * 512:r0 + (qc + 1) * 512], yb[:, :], r=[ybk], w=[YTt[h][c0g + qc]])
            P.barrier()
            sb.release(m1)

        if "B" in groups:
            l_ = l
            BQK = sb.alloc([128, 4, S], BF, "BQK")
            BQKk = [Tk() for _ in range(S // 256)]
            BVR = sb.alloc([128, NT, 4, 64], BF, "BVR")
            BVRk = [Tk() for _ in range(NT)]
            BO = sb.alloc([128, NT, 256], BF, "BO")
            BOk = [Tk() for _ in range(NT)]
            EB = sb.alloc([128, NT, 8], F32, "EB")
            ET = sb.alloc([128, NT, 8], F32, "ET")
            VSC = sb.alloc([128, NT, 8], F32, "VSC")
            EBk = [Tk() for _ in range(NT)]
            ngt = sb.alloc([128, 256], F32, "ngt")
            ngtk = Tk()
            P.dma("sp", ngt[:], Wd["mlstm_norm_g"][l_].partition_broadcast(128), w=[ngtk])
            mBp = sb.mark()
            WG, WGk = new_wg()
            load_wg(WG, WGk, 768, 1040)
            cw = sb.alloc([128, 4, 4], F32, "cwB")
            cwk = Tk()
            for c in range(4):
                for j in range(3):
                    P.dma("sp", cw[:, c, j:j + 1], Wd["mlstm_conv_w"][l_, j, c * 128:(c + 1) * 128].rearrange("(p o) -> p o", o=1), w=[cwk])
                P.dma("sp", cw[:, c, 3:4], Wd["mlstm_conv_b"][l_, c * 128:(c + 1) * 128].rearrange("(p o) -> p o", o=1), w=[cwk])
            gbias = sb.alloc([128, 16], F32, "gbias")
            gbiask = Tk()
            for dd in range(2):
                P.dma("sp", gbias[:, dd * 8:dd * 8 + 4], Wd["mlstm_igate_b"][l_, dd, :].partition_broadcast(128), w=[gbiask])
                P.dma("sp", gbias[:, dd * 8 + 4:dd * 8 + 8], Wd["mlstm_fgate_b"][l_, dd, :].partition_broadcast(128), w=[gbiask])
            t0_r = Ring([(sb.alloc([128, 256], F32, "t0B"), Tk()) for _ in range(2)])
            t1_r = Ring([(sb.alloc([128, 256], F32, "t1B"), Tk()) for _ in range(2)])
            for sw in range(S // 256):
                def bodyQ(c, sw=sw):
                    b = (sw * 4 + c) % 4

                    def fn(e, c=c, sw=sw, b=b, WG=WG):
                        for k in range(8):
                            ins = e.matmul(bank(b)[:, 0:258], lhsT=WG[:, k, c * 128:(c + 1) * 128], rhs=hT[:, k, sw * 256:sw * 256 + 258],
                                           start=(k == 0), stop=(k == 7))
                        return ins
                    P.op("pe", fn, r=hTk[sw * 2:sw * 2 + 2] + ([hTk[sw * 2 - 1]] if sw > 0 else []) + ([hTk[sw * 2 + 2]] if sw * 2 + 2 < NT else []) + [hTk[0], hTk[-1], WGk],
                         w=[PBK[b]])
                    t0, t0k = t0_r.next()
                    t1, t1k = t1_r.next()
                    P.op("act", lambda e, t0=t0, b=b, c=c: e.activation(out=t0[:], in_=bank(b)[:, 1:257], func=AF.Identity,
                                                                           scale=cw[:, c, 1:2], bias=cw[:, c, 3:4]), r=[PBK[b], cwk], w=[t0k])
                    P.op("dve", lambda e, t0=t0, t1=t1, b=b, c=c: e.scalar_tensor_tensor(out=t1[:], in0=bank(b)[:, 0:256], scalar=cw[:, c, 0:1],
                                                                                        in1=t0[:], op0=ALU.mult, op1=ALU.add), r=[PBK[b], cwk, t0k], w=[t1k])
                    P.op("dve", lambda e, t0=t0, t1=t1, b=b, c=c: e.scalar_tensor_tensor(out=t0[:], in0=bank(b)[:, 2:258], scalar=cw[:, c, 2:3],
                                                                                        in1=t1[:], op0=ALU.mult, op1=ALU.add), r=[PBK[b], cwk, t1k], w=[t0k])
                    P.op("act", lambda e, t0=t0, c=c, sw=sw: e.activation(out=BQK[:, c, sw * 256:(sw + 1) * 256], in_=t0[:], func=AF.Silu),
                         r=[t0k], w=[BQKk[sw]])
                interleaved(4, bodyQ)
            gs_r = Ring([(sb.alloc([128, 64], F32, "gsB"), Tk()) for _ in range(2)])
            def bodyB(i):
                b0 = 4 + (i % 2)
                bq = 6 + (i % 2)
                P.op("pe", proj_tok(WG, i, 512, 512, b0), r=[hTk[i], WGk], w=[PBK[b0]])
                P.op("pe", proj_tok(WG, i, 1024, 16, bq, off=0), r=[hTk[i], WGk], w=[PBK[bq]])
                gs, gsk = gs_r.next()
                P.op("dve", lambda e, gs=gs, bq=bq: e.tensor_tensor(out=gs[:, 0:16], in0=bank(bq)[:, 0:16], in1=gbias[:], op=ALU.add), r=[PBK[bq], gbiask], w=[gsk])
                gv = gs[:, 0:16].rearrange("p (d k h) -> p d k h", d=2, k=2)
                P.op("act", lambda e, gs=gs, gv=gv: e.activation(out=gs[:, 16:24].rearrange("p (d h) -> p d h", d=2), in_=gv[:, :, 1, :], func=AF.Sigmoid),
                     r=[gsk], w=[gsk])
                P.op("act", lambda e, gs=gs: e.activation(out=gs[:, 16:24], in_=gs[:, 16:24], func=AF.Ln), r=[gsk], w=[gsk])

                def fn(e, gs=gs, bq=bq):
                    e.matmul(bank(bq)[:, 16:20], lhsT=mf32[:], rhs=gs[:, 16:20], start=True, stop=True)
                    e.matmul(bank(bq)[:, 20:24], lhsT=mb32[:], rhs=gs[:, 20:24], start=True, stop=True)
                    return e.matmul(bank(bq)[:, 24:32], lhsT=ones32[:], rhs=gs[:, 16:24], start=True, stop=True)
                P.op("pe", fn, r=[gsk, mf32k, mb32k, ones32k], w=[PBK[bq]])
                P.op("act", lambda e, i=i, bq=bq: e.activation(out=EB[:, i, :], in_=bank(bq)[:, 16:24], func=AF.Exp), r=[PBK[bq]], w=[EBk[i]])
                P.op("act", lambda e, i=i, bq=bq: e.activation(out=ET[:, i, :], in_=bank(bq)[:, 24:32], func=AF.Exp), r=[PBK[bq]], w=[EBk[i]])
                P.op("dve", lambda e, gs=gs, gv=gv, bq=bq: e.tensor_tensor(out=gs[:, 24:32].rearrange("p (d h) -> p d h", d=2), in0=gv[:, :, 0, :],
                                                                     in1=bank(bq)[:, 16:24].rearrange("p (d h) -> p d h", d=2), op=ALU.subtract),
                     r=[gsk, PBK[bq]], w=[gsk])
                P.op("act", lambda e, gs=gs, i=i: e.activation(out=VSC[:, i, :], in_=gs[:, 24:32], func=AF.Exp, bias=cst[:, 2:3]), r=[gsk, cstk], w=[EBk[i]])
                P.op("act", lambda e, i=i, b0=b0: e.copy(out=BVR[:, i, :, :], in_=bank(b0)[:, 0:256].rearrange("p (h c) -> p h c", h=4)), r=[PBK[b0]], w=[BVRk[i]])
                P.op("act", lambda e, i=i, b0=b0: e.activation(out=BO[:, i, :], in_=bank(b0)[:, 256:512], func=AF.Sigmoid), r=[PBK[b0]], w=[BOk[i]])
            interleaved(NT, bodyB)
            P.barrier()
            sb.release(mBp)
            top = sb.mark()
            sb.release(mh)
            KT = sb.alloc([128, NT, 256], BF, "KT")
            KTk = [Tk() for _ in range(NT)]
            HF = sb.alloc([128, NT, 256], F32, "HF")
            HFk = [Tk() for _ in range(NT)]
            assert sb.off <= m1, "recurrence buffers overflow the hT region"
            sb.off = top
            for i in range(NT):
                b = tb[tbi[0] % 2]
                tbi[0] += 1
                pv = bank_bf(b)[:, 0:256].rearrange("p (k t) -> p k t", k=2)

                def fn(e, pv=pv, i=i):
                    for k in range(2):
                        ins = e.transpose(out=pv[:, k, :], in_=BQK[:, 2 + k, i * 128:(i + 1) * 128], identity=ident[:])
                    return ins
                P.op("pe", fn, r=[BQKk[i // 2], identk], w=[PBK[b]])
                P.op("act", lambda e, pv=pv, i=i: e.copy(out=KT[:, i, :].rearrange("p (k t) -> p k t", k=2), in_=pv), r=[PBK[b]], w=[KTk[i]])
            C32 = sb.alloc([128, 4, 65], F32, "C32")
            Cbfs = [(sb.alloc([128, 4, 65], BF, "Cbf"), Tk()) for _ in range(2)]
            C32k = Tk()
            va_r = Ring([(sb.alloc([128, 4, 65], BF, "vaB"), Tk()) for _ in range(3)])
            st_r = Ring([(sb.alloc([128, 4, 128], BF, "stB"), Tk()) for _ in range(3)])
            tt_r = Ring([(sb.alloc([128, 4, 65], F32, "ttB"), Tk()) for _ in range(2)])
            dn_r = Ring([(sb.alloc([128, 8], F32, "dnB"), Tk()) for _ in range(2)])
            hs_r = Ring([(sb.alloc([128, 256], F32, "hsB"), Tk()) for _ in range(2)])
            h2_r = Ring([(sb.alloc([128, 256], F32, "h2B"), Tk()) for _ in range(2)])
            yo_r = Ring([(sb.alloc([128, 256], BF, "yoB"), Tk()) for _ in range(2)])
            ystage = Ring([(sb.alloc([128, 2, 512], BF, "ystB"), Tk()) for _ in range(2)])
            for dd in range(2):
                order = list(range(NT)) if dd == 0 else list(range(NT - 1, -1, -1))
                msk, mskk = (mf32, mf32k) if dd == 0 else (mb32, mb32k)
                ystbox = [None]
                prepd = {}

                def prep(n, dd=dd, order=order, msk=msk, mskk=mskk, prepd=prepd):
                    i = order[n]
                    tsl = slice(i * 128, (i + 1) * 128)
                    sbk = n % 2
                    va, vak = va_r.next()
                    P.op("dve", lambda e: e.tensor_tensor(out=va[:, :, 0:64], in0=BVR[:, i, :, :],
                                                          in1=VSC[:, i, 4 * dd:4 * dd + 4].unsqueeze(2).broadcast_to([128, 4, 64]), op=ALU.mult),
                         r=[BVRk[i], EBk[i]], w=[vak])
                    P.op("pool", lambda e: e.tensor_copy(out=va[:, :, 64:65], in_=VSC[:, i, 4 * dd:4 * dd + 4].unsqueeze(2)),
                         r=[EBk[i]], w=[vak])

                    def fn(e):
                        for h in range(4):
                            pbs = slice(64 * (h % 2), 64 * (h % 2) + 64)
                            ins = e.matmul(PD[sbk][:, h % 2, (h // 2) * 128:(h // 2 + 1) * 128], lhsT=BQK[pbs, 2 + h // 2, tsl], rhs=BQK[pbs, h // 2, tsl], start=True, stop=True)
                        return ins
                    P.op("pe", fn, r=[BQKk[i // 2]], w=[PBK[2 * sbk], PBK[2 * sbk + 1]])
                    stt, sttk = st_r.next()
                    P.op("dve", lambda e: e.tensor_tensor(out=stt[:].rearrange("p (hp e) l -> p hp e l", hp=2),
                                                          in0=PD[sbk][:, :, 0:256].rearrange("p e (hp l) -> p hp e l", hp=2),
                                                          in1=msk[:].unsqueeze(1).unsqueeze(1).broadcast_to([128, 2, 2, 128]), op=ALU.mult),
                         r=[PBK[2 * sbk], PBK[2 * sbk + 1], mskk], w=[sttk])
                    prepd[n] = (va, vak, stt, sttk)

                def step(n, dd=dd, order=order, prepd=prepd, ystbox=ystbox):
                    i = order[n]
                    tsl = slice(i * 128, (i + 1) * 128)
                    va, vak, stt, sttk = prepd.pop(n)
                    Cprev, Cprevk = Cbfs[(n + 1) % 2]
                    Cnew, Cnewk = Cbfs[n % 2]
                    cbk = 6 + n % 2
                    cv = bank(cbk)[:, 0:260].rearrange("p (h c) -> p h c", h=4)
                    if n < NT - 1:
                        def fn3(e):
                            for h in range(4):
                                ins = e.matmul(cv[:, h, :], lhsT=KT[:, i, (h // 2) * 128:(h // 2 + 1) * 128], rhs=va[:, h, :], start=True, stop=True)
                            return ins
                        P.op("pe", fn3, r=[KTk[i], vak], w=[PBK[cbk]])
                    obk = 4 + n % 2
                    ov = bank(obk)[:, 0:260].rearrange("p (h c) -> p h c", h=4)

                    def fn2(e):
                        for h in range(4):
                            ins = e.matmul(ov[:, h, :], lhsT=stt[:, h, :], rhs=va[:, h, :], start=True, stop=(n == 0))
                            if n > 0:
                                ins = e.matmul(ov[:, h, :], lhsT=BQK[:, h // 2, tsl], rhs=Cprev[:, h, :], start=False, stop=True)
                        return ins
                    P.op("pe", fn2, r=[sttk, vak, BQKk[i // 2]] + ([Cprevk] if n > 0 else []), w=[PBK[obk]])
                    if n < NT - 1:
                        etb = ET[:, i, 4 * dd:4 * dd + 4].unsqueeze(2).broadcast_to([128, 4, 65])
                        if n == 0:
                            P.op("dve", lambda e: e.tensor_tensor(out=C32[:], in0=cv, in1=etb, op=ALU.mult), r=[PBK[cbk], EBk[i]], w=[C32k])
                        else:
                            P.op("dve", lambda e: e.tensor_tensor(out=C32[:], in0=cv, in1=C32[:], op=ALU.add), r=[PBK[cbk], C32k], w=[C32k])
                            P.op("dve", lambda e: e.tensor_tensor(out=C32[:], in0=C32[:], in1=etb, op=ALU.mult), r=[C32k, EBk[i]], w=[C32k])
                        P.op("pool", lambda e: e.tensor_tensor(out=Cnew[:], in0=C32[:], in1=hmask[:], op=ALU.mult), r=[C32k, hmaskk], w=[Cnewk])
                    tt, ttk = tt_r.next()
                    P.op("dve", lambda e: e.tensor_tensor(out=tt[:], in0=ov, in1=EB[:, i, 4 * dd:4 * dd + 4].unsqueeze(2).broadcast_to([128, 4, 65]),
                                                          op=ALU.mult), r=[PBK[obk], EBk[i]], w=[ttk])
                    dn, dnk = dn_r.next()
                    P.op("dve", lambda e: e.scalar_tensor_tensor(out=dn[:, 0:4], in0=tt[:, :, 64], scalar=-1.0, in1=tt[:, :, 64],
                                                                 op0=ALU.mult, op1=ALU.max), r=[ttk], w=[dnk])
                    P.op("dve", lambda e: e.tensor_scalar(out=dn[:, 0:4], in0=dn[:, 0:4], scalar1=1.0, scalar2=None, op0=ALU.max), r=[dnk], w=[dnk])
                    P.op("dve", lambda e: e.reciprocal(out=dn[:, 4:8], in_=dn[:, 0:4]), r=[dnk], w=[dnk])
                    if dd == 0:
                        P.op("dve", lambda e: e.tensor_tensor(out=HF[:, i, :].rearrange("p (h c) -> p h c", h=4), in0=tt[:, :, 0:64],
                                                              in1=dn[:, 4:8].unsqueeze(2).broadcast_to([128, 4, 64]), op=ALU.mult),
                             r=[ttk, dnk], w=[HFk[i]])
                        return
                    hs, hsk = hs_r.next()
                    h2, h2k = h2_r.next()
                    P.op("dve", lambda e: e.tensor_tensor(out=hs[:].rearrange("p (h c) -> p h c", h=4), in0=tt[:, :, 0:64],
                                                          in1=dn[:, 4:8].unsqueeze(2).broadcast_to([128, 4, 64]), op=ALU.mult),
                         r=[ttk, dnk], w=[hsk])
                    P.op("dve", lambda e: e.tensor_tensor(out=hs[:], in0=hs[:], in1=HF[:, i, :], op=ALU.add), r=[hsk, HFk[i]], w=[hsk])
                    st, stk = stat.next()
                    P.op("dve", lambda e: e.tensor_tensor(out=h2[:], in0=hs[:], in1=hs[:], op=ALU.mult), r=[hsk], w=[h2k])
                    P.op("dve", lambda e: e.tensor_reduce(out=st[:, 0:4], in_=h2[:].rearrange("p (h c) -> p h c", h=4), axis=AX.X, op=ALU.add),
                         r=[h2k], w=[stk])
                    rstd_from_ss(st, stk, 4, 1.0 / 64)
                    P.op("dve", lambda e: e.tensor_tensor(out=h2[:].rearrange("p (h c) -> p h c", h=4), in0=hs[:].rearrange("p (h c) -> p h c", h=4),
                                                          in1=st[:, 4:8].unsqueeze(2).broadcast_to([128, 4, 64]), op=ALU.mult),
                         r=[hsk, stk], w=[h2k])
                    P.op("dve", lambda e: e.tensor_tensor(out=hs[:], in0=h2[:], in1=ngt[:], op=ALU.mult), r=[h2k, ngtk], w=[hsk])
                    yo, yok = yo_r.next()
                    P.op("dve", lambda e: e.tensor_tensor(out=yo[:], in0=hs[:], in1=BO[:, i, :], op=ALU.mult), r=[hsk, BOk[i]], w=[yok])
                    if n % 4 == 0:
                        ystbox[0] = ystage.next()
                    ysb, ysk = ystbox[0]
                    j = i % 4
                    transpose_to(yo, yok, 2, lambda: ysb[:, :, j * 128:(j + 1) * 128], [ysk])
                    if n % 4 == 3:
                        qc = i // 4
                        for k in range(2):
                            P.dma("sp", YT[256 + 128 * k:256 + 128 * (k + 1), r0 + qc * 512:r0 + (qc + 1) * 512], ysb[:, k, :], r=[ysk],
                                  w=[YTt[4 + 2 * k][c0g + qc], YTt[5 + 2 * k][c0g + qc]])

                prep(0)
                for n in range(NT):
                    lists = []
                    if n + 1 < NT:
                        lists.append(P.capture(lambda n=n: prep(n + 1)))
                    lists.insert(0, P.capture(lambda n=n: step(n)))
                    P.emit_interleaved(lists)
        P.barrier()
        sb.release(m0)

    def mixer_out(l, si, w_out_sb, w_out_k):
        S = seq_lens[si]
        r0 = SOFF[si]
        c0g = r0 // 512
        m0 = sb.mark()
        yts = Ring([(sb.alloc([128, 8, 512], BF, "yts"), Tk()) for _ in range(2)])
        xo_r = Ring([(sb.alloc([128, D], F32, "xo"), Tk()) for _ in range(2)])
        if l == 0:
            src, srctk, srow0 = xs[si], None, 0
        else:
            src, srctk, srow0 = XC, XCt, r0
        for qc in range(S // 512):
            yt, ytk = yts.next()
            P.dma("sp", yt[:], YT[:, r0 + qc * 512:r0 + (qc + 1) * 512].rearrange("(k p) t -> p k t", p=128),
                  r=[YTt[rg][c0g + qc] for rg in range(16)], w=[ytk])
            def bodyM(tt, qc=qc, yt=yt, ytk=ytk):
                ti = qc * 4 + tt
                xt, xtk = xring.next()
                P.dma("sp", xt[:], src[srow0 + ti * 128:srow0 + (ti + 1) * 128, :], r=[srctk[srow0 // 128 + ti]] if srctk else [], w=[xtk])
                xo, xok = xo_r.next()
                for nh in range(2):
                    b = (tt * 2 + nh) % 4
                    def fn(e, yt=yt, tt=tt, nh=nh, b=b):
                        for k in range(8):
                            ins = e.matmul(bank(b)[:, :], lhsT=yt[:, k, tt * 128:(tt + 1) * 128], rhs=w_out_sb[:, k, nh * 512:(nh + 1) * 512],
                                           start=(k == 0), stop=(k == 7))
                        return ins
                    P.op("pe", fn, r=[ytk, w_out_k], w=[PBK[b]])
                    P.op("dve", lambda e, xo=xo, xt=xt, nh=nh, b=b: e.tensor_tensor(out=xo[:, nh * 512:(nh + 1) * 512], in0=bank(b)[:, :],
                                                                                    in1=xt[:, nh * 512:(nh + 1) * 512], op=ALU.add), r=[PBK[b], xtk], w=[xok])
                P.dma("pool", XA[r0 + ti * 128:r0 + (ti + 1) * 128, :], xo[:], r=[xok], w=[XAt[r0 // 128 + ti]])
            interleaved(4, bodyM)
        P.barrier()
        sb.release(m0)

    def cross(l, si, Wq, Wkv, Wo, wk):
        S = seq_lens[si]
        NT = S // 128
        r0 = SOFF[si]
        m0 = sb.mark()
        alloc_aux(2)
        hT = sb.alloc([128, 8, S + 2], BF, "hTx")
        hTk = [Tk() for _ in range(NT)]
        build_hT(XA, XAt, r0, S, Wd["norm_x_g"][l], hT, hTk)
        mT = sb.alloc([128, 8, 256], BF, "mT")
        mTk = Tk()
        g, gk = load_gain(Wd["norm_mem_g"][l])
        for i in range(2):
            xt, xtk = xring.next()
            P.dma("sp", xt[:], msrc[si][i * 128:(i + 1) * 128, :], w=[xtk])
            h, hk = norm_rows(xt, xtk, g, gk)
            transpose_to(h, hk, 8, lambda i=i: mT[:, :, i * 128:(i + 1) * 128], [mTk])
        XKT = sb.alloc([128, 2, 256], BF, "XKT")
        XKTk = Tk()
        XV = sb.alloc([128, 2, 4, 65], BF, "XV")
        XVk = Tk()
        P.op("pool", lambda e: e.memset(XV[:, :, :, 64:65], 1.0), w=[XVk])
        for hp in range(2):
            def fn(e, hp=hp):
                for k in range(8):
                    ins = e.matmul(bank(hp)[:, 0:256], lhsT=Wkv[:, k, hp * 128:(hp + 1) * 128], rhs=mT[:, k, :], start=(k == 0), stop=(k == 7))
                return ins
            P.op("pe", fn, r=[wk, mTk], w=[PBK[hp]])
            P.op("act", lambda e, hp=hp: e.copy(out=XKT[:, hp, :], in_=bank(hp)[:, 0:256]), r=[PBK[hp]], w=[XKTk])
        for mt in range(2):
            def fn(e, mt=mt):
                for k in range(8):
                    ins = e.matmul(bank(2 + mt)[:, 0:256], lhsT=mT[:, k, mt * 128:(mt + 1) * 128], rhs=Wkv[:, k, 256:512], start=(k == 0), stop=(k == 7))
                return ins
            P.op("pe", fn, r=[wk, mTk], w=[PBK[2 + mt]])
            P.op("act", lambda e, mt=mt: e.copy(out=XV[:, mt, :, 0:64], in_=bank(2 + mt)[:, 0:256].rearrange("p (h c) -> p h c", h=4)),
                 r=[PBK[2 + mt]], w=[XVk])
        XQT_r = Ring([(sb.alloc([128, 2, 512], BF, "XQT"), Tk()) for _ in range(2)])
        XO_r = Ring([(sb.alloc([64, 4, 512], BF, "XO"), Tk()) for _ in range(2)])
        pbuf = Ring([(sb.alloc([128, 2, 512], BF, "pbufx"), Tk()) for _ in range(3)])
        xo_r = Ring([(sb.alloc([128, D], F32, "xox"), Tk()) for _ in range(2)])
        def attn(qc):
            xq, xqk = XQT_r.next()
            for hp in range(2):
                b = 6 + hp

                def fn(e, hp=hp, b=b):
                    for k in range(8):
                        ins = e.matmul(bank(b)[:, :], lhsT=Wq[:, k, hp * 128:(hp + 1) * 128], rhs=hT[:, k, 1 + qc * 512:1 + (qc + 1) * 512],
                                       start=(k == 0), stop=(k == 7))
                    return ins
                P.op("pe", fn, r=[wk] + hTk[qc * 4:(qc + 1) * 4], w=[PBK[b]])
                P.op("act", lambda e, hp=hp, b=b: e.activation(out=xq[:, hp, :], in_=bank(b)[:, :], func=AF.Copy, scale=0.125), r=[PBK[b]], w=[xqk])
            xo_t, xo_tk = XO_r.next()

            def s_mm(h):
                pbs = slice(64 * (h % 2), 64 * (h % 2) + 64)
                sd = (h % 2) * 2

                def fn(e):
                    for mt in range(2):
                        ins = e.matmul(bank(sd + mt)[:, :], lhsT=XKT[pbs, h // 2, mt * 128:(mt + 1) * 128], rhs=xq[pbs, h // 2, :], start=True, stop=True)
                    return ins
                P.op("pe", fn, r=[XKTk, xqk], w=[PBK[sd], PBK[sd + 1]])

            s_mm(0)
            stq = []
            for h in range(4):
                sd = (h % 2) * 2
                if h + 1 < 4:
                    s_mm(h + 1)
                pb_, pbk_ = pbuf.next()
                P.op("act", lambda e, pb_=pb_, sd=sd: e.activation(out=pb_[:], in_=PD[sd // 2][:], func=AF.Exp), r=[PBK[sd], PBK[sd + 1]], w=[pbk_])
                ob = 4 + h % 2

                def fn2(e, h=h, pb_=pb_, ob=ob):
                    for mt in range(2):
                        ins = e.matmul(bank(ob)[0:65, :], lhsT=XV[:, mt, h, :], rhs=pb_[:, mt, :], start=(mt == 0), stop=(mt == 1))
                    return ins
                P.op("pe", fn2, r=[XVk, pbk_], w=[PBK[ob]])
                stq.append((h, norm_OT_a(ob, 512)))
                if len(stq) > 1:
                    h0, st0 = stq.pop(0)
                    norm_OT_b(st0, 512, xo_t[:, h0, :], [xo_tk])
            for h0, st0 in stq:
                norm_OT_b(st0, 512, xo_t[:, h0, :], [xo_tk])
            return xo_t, xo_tk

        def outproj(qc, xo_t, xo_tk):
            for tt in range(4):
                ti = qc * 4 + tt
                xt, xtk = xring.next()
                P.dma("sp", xt[:], XA[r0 + ti * 128:r0 + (ti + 1) * 128, :], r=[XAt[r0 // 128 + ti]], w=[xtk])
                xo, xok = xo_r.next()
                for nh in range(2):
                    b = 6 + nh

                    def fn(e, tt=tt, nh=nh, b=b):
                        for h in range(4):
                            ins = e.matmul(bank(b)[:, :], lhsT=xo_t[:, h, tt * 128:(tt + 1) * 128], rhs=Wo[:, h, nh * 512:(nh + 1) * 512],
                                           start=(h == 0), stop=(h == 3))
                        return ins
                    P.op("pe", fn, r=[xo_tk, wk], w=[PBK[b]])
                    P.op("dve", lambda e, xo=xo, xt=xt, nh=nh, b=b: e.tensor_tensor(out=xo[:, nh * 512:(nh + 1) * 512], in0=bank(b)[:, :],
                                                                                    in1=xt[:, nh * 512:(nh + 1) * 512], op=ALU.add), r=[PBK[b], xtk], w=[xok])
                P.dma("pool", XB[r0 + ti * 128:r0 + (ti + 1) * 128, :], xo[:], r=[xok], w=[XBt[r0 // 128 + ti]])

        prev = None
        for qc in range(S // 512):
            cur = attn(qc)
            if prev is not None:
                outproj(qc - 1, *prev)
            prev = cur
        outproj(S // 512 - 1, *prev)
        P.barrier()
        sb.release(m0)

    NCH = 22
    CWD = [128] * 21 + [64]

    def ffn(l, si, Wup, Wdn, fcw, wk, last):
        S = seq_lens[si]
        NW = S // 256
        r0 = SOFF[si]
        m0 = sb.mark()
        alloc_aux(2, need_norm=False)
        g, gk = load_gain(Wd["norm_ffn_g"][l])
        if last:
            gf, gfk = load_gain(Wd["final_norm_g"])
        HW = [(sb.alloc([128, 8, 258], BF, "HW"), Tk()) for _ in range(3)]
        AT_r = Ring([(sb.alloc([128, NCH, 256], BF, "AT"), Tk()) for _ in range(1)])
        t0_r = Ring([(sb.alloc([128, 256], F32, "t0F"), Tk()) for _ in range(3)])
        t1_r = Ring([(sb.alloc([128, 256], F32, "t1F"), Tk()) for _ in range(3)])
        xo_r = Ring([(sb.alloc([128, D], F32, "xoF"), Tk()) for _ in range(2)])

        def make_window(w):
            hw, hwk = HW[w % 3]

            def bodyW(j):
                ti = w * 2 + j
                xt, xtk = xring.next()
                P.dma("sp", xt[:], XB[r0 + ti * 128:r0 + (ti + 1) * 128, :], r=[XBt[r0 // 128 + ti]], w=[xtk])
                h, hk = norm_rows(xt, xtk, g, gk)
                transpose_to(h, hk, 8, lambda hw=hw, j=j: hw[:, :, 1 + j * 128:1 + (j + 1) * 128], [hwk])
            interleaved(2, bodyW)

        make_window(0)
        for w in range(NW):
            hw, hwk = HW[w % 3]
            if w + 1 < NW:
                make_window(w + 1)
                hn, hnk = HW[(w + 1) % 3]
                P.op("pool", lambda e, hw=hw, hn=hn: e.tensor_copy(out=hw[:, :, 257:258], in_=hn[:, :, 1:2]), r=[hnk], w=[hwk])
            else:
                P.op("pool", lambda e, hw=hw: e.memset(hw[:, :, 257:258], 0.0), w=[hwk])
            if w > 0:
                hp_, hpk = HW[(w - 1) % 3]
                P.op("pool", lambda e, hw=hw, hp_=hp_: e.tensor_copy(out=hw[:, :, 0:1], in_=hp_[:, :, 256:257]), r=[hpk], w=[hwk])
            else:
                P.op("pool", lambda e, hw=hw: e.memset(hw[:, :, 0:1], 0.0), w=[hwk])
            at, atk = AT_r.next()

            def bodyF(c, hw=hw, hwk=hwk, at=at, atk=atk):
                cwd = CWD[c]
                bg = (c % 3) * 2
                bv = bg + 1
                def fn(e, c=c, cwd=cwd, bg=bg, hw=hw):
                    for k in range(8):
                        ins = e.matmul(bank(bg)[0:cwd, 0:258], lhsT=Wup[:, k, c * 128:c * 128 + cwd], rhs=hw[:, k, 0:258], start=(k == 0), stop=(k == 7))
                    return ins
                P.op("pe", fn, r=[hwk, wk["g"][c // 6]], w=[PBK[bg]])

                def fnv(e, c=c, cwd=cwd, bv=bv, hw=hw):
                    for k in range(8):
                        ins = e.matmul(bank(bv)[0:cwd, 0:256], lhsT=Wup[:, k, DFF + c * 128:DFF + c * 128 + cwd], rhs=hw[:, k, 1:257], start=(k == 0), stop=(k == 7))
                    return ins
                P.op("pe", fnv, r=[hwk, wk["g"][c // 6]], w=[PBK[bv]])
                t0, t0k = t0_r.next()
                t1, t1k = t1_r.next()
                P.op("act", lambda e, t0=t0, bg=bg, c=c, cwd=cwd: e.activation(out=t0[0:cwd, :], in_=bank(bg)[0:cwd, 1:257], func=AF.Identity,
                                                                                scale=fcw[0:cwd, c, 1:2], bias=fcw[0:cwd, c, 3:4]), r=[PBK[bg], wk["c"]], w=[t0k])
                P.op("dve", lambda e, t0=t0, t1=t1, bg=bg, c=c, cwd=cwd: e.scalar_tensor_tensor(out=t1[0:cwd, :], in0=bank(bg)[0:cwd, 0:256], scalar=fcw[0:cwd, c, 0:1],
                                                                                               in1=t0[0:cwd, :], op0=ALU.mult, op1=ALU.add), r=[PBK[bg], wk["c"], t0k], w=[t1k])
                P.op("dve", lambda e, t0=t0, t1=t1, bg=bg, c=c, cwd=cwd: e.scalar_tensor_tensor(out=t0[0:cwd, :], in0=bank(bg)[0:cwd, 2:258], scalar=fcw[0:cwd, c, 2:3],
                                                                                               in1=t1[0:cwd, :], op0=ALU.mult, op1=ALU.add), r=[PBK[bg], wk["c"], t1k], w=[t0k])
                P.op("act", lambda e, t0=t0, t1=t1, cwd=cwd: e.activation(out=t1[0:cwd, :], in_=t0[0:cwd, :], func=AF.Silu), r=[t0k], w=[t1k])
                P.op("dve", lambda e, t1=t1, at=at, bv=bv, c=c, cwd=cwd: e.tensor_tensor(out=at[0:cwd, c, :], in0=t1[0:cwd, :], in1=bank(bv)[0:cwd, 0:256], op=ALU.mult),
                     r=[t1k, PBK[bv]], w=[atk])
            interleaved(NCH, bodyF)
            for tt in range(2):
                ti = w * 2 + tt
                xt, xtk = xring.next()
                P.dma("sp", xt[:], XB[r0 + ti * 128:r0 + (ti + 1) * 128, :], r=[XBt[r0 // 128 + ti]], w=[xtk])
                xo, xok = xo_r.next()
                for nh in range(2):
                    b = 6 + (tt * 2 + nh) % 2
                    def fn(e, at=at, tt=tt, nh=nh, b=b):
                        for c in range(NCH):
                            cwd = CWD[c]
                            ins = e.matmul(bank(b)[:, :], lhsT=at[0:cwd, c, tt * 128:(tt + 1) * 128], rhs=Wdn[0:cwd, c, nh * 512:(nh + 1) * 512],
                                           start=(c == 0), stop=(c == NCH - 1))
                        return ins
                    P.op("pe", fn, r=[atk, wk["d"]], w=[PBK[b]])
                    P.op("dve", lambda e, xo=xo, xt=xt, nh=nh, b=b: e.tensor_tensor(out=xo[:, nh * 512:(nh + 1) * 512], in0=bank(b)[:, :],
                                                                                    in1=xt[:, nh * 512:(nh + 1) * 512], op=ALU.add), r=[PBK[b], xtk], w=[xok])
                if not last:
                    P.dma("pool", XC[r0 + ti * 128:r0 + (ti + 1) * 128, :], xo[:], r=[xok], w=[XCt[r0 // 128 + ti]])
                else:
                    st, stk = stat.next()
                    P.op("dve", lambda e, xo=xo, st=st: e.scalar_tensor_tensor(out=junk[:], in0=xo[:], scalar=1.0, in1=xo[:], op0=ALU.mult, op1=ALU.mult,
                                                                                accum_out=st[:, 0:1]), r=[xok], w=[junkk, stk])
                    rstd_from_ss(st, stk, 1, 1.0 / D)
                    yo, yok = xring.next()
                    P.op("dve", lambda e, xo=xo, st=st, yo=yo: e.scalar_tensor_tensor(out=yo[:], in0=xo[:], scalar=st[:, 1:2], in1=gf[:],
                                                                                       op0=ALU.mult, op1=ALU.mult), r=[xok, stk, gfk], w=[yok])
                    P.dma("pool", ys[si][ti * 128:(ti + 1) * 128, :], yo[:], r=[yok])
        P.barrier()
        sb.release(m0)

    def ffn_prefetch(l):
        Wdn = sb.alloc_top([128, NCH, D], BF, "Wdn")
        fcw = sb.alloc_top([128, NCH, 4], F32, "fcw")
        wk = {"g": [Tk() for _ in range(4)], "d": Tk(), "c": Tk()}
        for c in range(NCH):
            cwd = CWD[c]
            for j in range(3):
                P.dma("sp", fcw[0:cwd, c, j:j + 1], Wd["ffn_conv_w"][l, j, c * 128:c * 128 + cwd].rearrange("(p o) -> p o", o=1), w=[wk["c"]])
            P.dma("sp", fcw[0:cwd, c, 3:4], Wd["ffn_conv_b"][l, c * 128:c * 128 + cwd].rearrange("(p o) -> p o", o=1), w=[wk["c"]])
        P.dma("pool", Wdn[:, 0:21, :], Wd["w_ffn_down"][l][0:21 * 128, :].rearrange("(c p) n -> p c n", p=128), w=[wk["d"]])
        P.dma("pool", Wdn[0:64, 21, :], Wd["w_ffn_down"][l][21 * 128:DFF, :], w=[wk["d"]])
        return Wdn, fcw, wk

    ffn_pre = None
    for l in range(depth):
        if "mix" in phases:
            for si in range(NS):
                mixers(l, si)
        if "mixout" in phases:
            m0 = sb.mark()
            wout = sb.alloc([128, 8, D], BF, "wout")
            woutk = Tk()
            P.dma("pool", wout[:], Wd["w_out"][l].rearrange("(k p) n -> p k n", p=128), w=[woutk])
            for si in range(NS):
                mixer_out(l, si, wout, woutk)
            sb.release(m0)
        if "cross" in phases:
            m0 = sb.mark()
            Wq = sb.alloc([128, 8, 256], BF, "Wq")
            Wkv = sb.alloc([128, 8, 512], BF, "Wkv")
            Wo = sb.alloc([64, 4, D], BF, "Wo")
            wk = Tk()
            P.dma("pool", Wq[:], Wd["w_xq"][l].rearrange("(k p) n -> p k n", p=128), w=[wk])
            P.dma("pool", Wkv[:], Wd["w_xkv"][l].rearrange("(k p) n -> p k n", p=128), w=[wk])
            P.dma("pool", Wo[:], Wd["w_xo"][l].rearrange("(h p) n -> p h n", p=64), w=[wk])
            order = sorted(range(NS), key=lambda si: -seq_lens[si])
            for si in order:
                if si == order[-1] and "ffn" in phases:
                    ffn_pre = ffn_prefetch(l)
                cross(l, si, Wq, Wkv, Wo, wk)
            sb.release(m0)
        if "ffn" in phases:
            m0 = sb.mark()
            if ffn_pre is None:
                ffn_pre = ffn_prefetch(l)
            Wdn, fcw, wk = ffn_pre
            ffn_pre = None
            Wup = sb.alloc([128, 8, 2 * DFF], BF, "Wup")
            for g in range(4):
                c0 = 768 * g
                c1 = min(768 * (g + 1), DFF)
                for k in range(8):
                    P.dma("pool", Wup[:, k, c0:c1], Wd["w_ffn_up"][l][k * 128:(k + 1) * 128, c0:c1], w=[wk["g"][g]])
                    P.dma("pool", Wup[:, k, DFF + c0:DFF + c1], Wd["w_ffn_up"][l][k * 128:(k + 1) * 128, DFF + c0:DFF + c1], w=[wk["g"][g]])
            for si in range(NS):
                ffn(l, si, Wup, Wdn, fcw, wk, last=(l == depth - 1))
            sb.release(m0)
            sb.reset_top()
    P.barrier()
    P.replay()
    return nc, P


_CACHE = {}


def kernel(**inputs):
    seq_lens = (2048, 2048, 4096)
    ncores = 8
    if "nc" not in _CACHE:
        _CACHE["nc"] = build_program(seq_lens)[0]
    nc = _CACHE["nc"]
    consts = host_consts(4096)
    xp = np.asarray(inputs["x_prompt"], dtype=np.float32)
    xsm = np.asarray(inputs["x_sample"], dtype=np.float32)
    mp = np.asarray(inputs["mem_prompt"], dtype=np.float32)
    msm = np.asarray(inputs["mem_sample"], dtype=np.float32)
    in_maps = []
    for c in range(ncores):
        m = {"xs0": np.ascontiguousarray(xp[2 * c]), "xs1": np.ascontiguousarray(xp[2 * c + 1]), "xs2": np.ascontiguousarray(xsm[c]),
             "ms0": np.ascontiguousarray(mp[2 * c]), "ms1": np.ascontiguousarray(mp[2 * c + 1]), "ms2": np.ascontiguousarray(msm[c])}
        for n in WSHAPES:
            m[n] = np.ascontiguousarray(np.asarray(inputs[n], dtype=np.float32))
        m.update(consts)
        in_maps.append(m)
    res = run_bass_kernel_spmd(nc, in_maps, core_ids=list(range(ncores)))
    yp = np.empty((16, 2048, D), np.float32)
    ysm = np.empty((8, 4096, D), np.float32)
    for c in range(ncores):
        r = res.results[c]
        yp[2 * c] = r["ys0"]
        yp[2 * c + 1] = r["ys1"]
        ysm[c] = r["ys2"]
    return (yp, ysm)
```

```python
import numpy as np
import ml_dtypes
import concourse.bass as bass
import concourse.mybir as mybir
from concourse.bass_utils import run_bass_kernel_spmd

F32 = mybir.dt.float32
BF = mybir.dt.bfloat16
ALU = mybir.AluOpType
AF = mybir.ActivationFunctionType
AX = mybir.AxisListType

D = 1024
INW = 2576
DFF = 2752
NMEM = 256
DEPTH = 2
EPS = 1e-6
NEG = -30000.0
NRING = 24
SAME_ENG_SYNC = True
SB_LO = 16512
SB_HI = 229344


class Tk:
    __slots__ = ("w", "rs", "name")

    def __init__(self, name=""):
        self.w = None
        self.rs = {}
        self.name = name


class Prog:
    CE = ("pe", "act", "dve", "pool")
    ALLE = ("pe", "act", "dve", "pool", "sp")

    def __init__(self, nc):
        self.nc = nc
        self.sem = {}
        for e in self.CE:
            self.sem[("c", e)] = nc.alloc_semaphore("sc_" + e)
        self.cnt = {e: 0 for e in self.CE}
        self.queues = ("sp", "pool")
        self.ring_i = {q: 0 for q in self.queues}
        for q in self.queues:
            for i in range(NRING):
                self.sem[("r", q, i)] = nc.alloc_semaphore("sr_%s_%d" % (q, i))
        self.ring_cnt = {}
        self.waited = {e: {} for e in self.ALLE}
        self.prog = {e: [] for e in self.ALLE}
        self.n_ins = 0
        self._cap = None

    def _deps(self, eng, r, w, extra=()):
        need = {}

        def add(ev):
            if ev is None:
                return
            k, v = ev
            if need.get(k, 0) < v:
                need[k] = v
        own = None
        for t in r:
            add(t.w)
        for t in w:
            if t.w is not None and t.w[0] != own:
                add(t.w)
            for k, v in t.rs.items():
                if k != own:
                    add((k, v))
        for ev in extra:
            add(ev)
        out = []
        wd = self.waited[eng]
        for k, v in need.items():
            if k == ("c", eng):
                if eng == "pe" or not SAME_ENG_SYNC:
                    continue
            if wd.get(k, 0) >= v:
                continue
            wd[k] = v
            out.append((k, v))
        return out

    def _mark(self, ev, r, w):
        k, v = ev
        for t in r:
            if t.rs.get(k, 0) < v:
                t.rs[k] = v
        for t in w:
            t.w = ev
            t.rs = {}

    def capture(self, fn):
        self._cap = []
        fn()
        lst, self._cap = self._cap, None
        return lst

    def emit_interleaved(self, lists):
        idx = [0] * len(lists)
        left = sum(len(x) for x in lists)
        while left:
            for k, lst in enumerate(lists):
                if idx[k] < len(lst):
                    it = lst[idx[k]]
                    idx[k] += 1
                    left -= 1
                    if it[0] == "op":
                        self.op(it[1], it[2], it[3], it[4])
                    else:
                        self.dma(it[1], it[2], it[3], it[4], it[5], **it[6])

    def op(self, eng, fn, r=(), w=()):
        if self._cap is not None:
            self._cap.append(("op", eng, fn, tuple(r), tuple(w)))
            return None
        waits = self._deps(eng, r, w)
        self.cnt[eng] += 1
        k = ("c", eng)
        ev = (k, self.cnt[eng])
        self.prog[eng].append((waits, fn, (k, 1)))
        self._mark(ev, r, w)
        self.n_ins += 1
        return ev

    def dma(self, q, out, in_, r=(), w=(), **kw):
        if self._cap is not None:
            self._cap.append(("dma", q, out, in_, tuple(r), tuple(w), kw))
            return None
        i = self.ring_i[q]
        self.ring_i[q] = (i + 1) % NRING
        k = ("r", q, i)
        c = self.ring_cnt.get(k, 0)
        extra = [(k, 16 * c)] if c > 0 else []
        waits = self._deps(q, r, w, extra)
        self.ring_cnt[k] = c + 1
        ev = (k, 16 * (c + 1))

        def fn(e, out=out, in_=in_, kw=kw):
            return e.dma_start(out=out, in_=in_, **kw)
        self.prog[q].append((waits, fn, (k, 16)))
        self._mark(ev, r, w)
        self.n_ins += 1
        return ev

    def barrier(self):
        evs = [(("c", e), self.cnt[e]) for e in self.CE if self.cnt[e] > 0]
        evs += [(k, 16 * c) for k, c in self.ring_cnt.items()]
        for eng in self.ALLE:
            waits = []
            wd = self.waited[eng]
            for k, v in evs:
                if wd.get(k, 0) >= v:
                    continue
                wd[k] = v
                waits.append((k, v))
            if waits:
                self.prog[eng].append((waits, None, None))

    def replay(self):
        nc = self.nc
        sem = self.sem
        prog = self.prog

        def run(name, e):
            for waits, fn, inc in prog[name]:
                for k, v in waits:
                    e.wait_ge(sem[k], v)
                if fn is not None:
                    ins = fn(e)
                    ins.then_inc(sem[inc[0]], inc[1])
        with nc.Block() as block:
            @block.tensor
            def _(e):
                run("pe", e)

            @block.scalar
            def _(e):
                run("act", e)

            @block.vector
            def _(e):
                run("dve", e)

            @block.gpsimd
            def _(e):
                run("pool", e)

            @block.sync
            def _(e):
                run("sp", e)


class SBAlloc:
    def __init__(self, nc):
        self.nc = nc
        self.off = SB_LO
        self.n = 0

    def alloc(self, shape, dt, name="t"):
        sz = 1
        for s in shape[1:]:
            sz *= s
        sz *= 2 if dt == BF else 4
        off = (self.off + 63) // 64 * 64
        assert off + sz <= SB_HI, ("SBUF overflow", name, off, sz)
        self.off = off + sz
        self.n += 1
        return self.nc.alloc_sbuf_tensor_at("%s_%d" % (name, self.n), list(shape), dt, offset=off)

    def mark(self):
        return self.off

    def release(self, m):
        self.off = m


def ssl(start, count, step):
    return slice(start, start + (count - 1) * step + 1, step)


class Ring:
    def __init__(self, items):
        self.items = items
        self.i = 0

    def next(self):
        it = self.items[self.i % len(self.items)]
        self.i += 1
        return it


def host_consts(smax):
    c = {}
    c["c_ident"] = np.eye(128, dtype=np.float32).astype(ml_dtypes.bfloat16)
    i = np.arange(128)[:, None]
    j = np.arange(128)[None, :]
    c["c_mf"] = (i <= j).astype(np.float32)
    c["c_mb"] = (i >= j).astype(np.float32)
    ma = np.where(i >= j, 0.0, NEG).astype(np.float32)
    mb = np.where(i <= j, 0.0, NEG).astype(np.float32)
    m1 = np.zeros((128, 128), np.float32)
    m1[:64] = ma[64:]
    a01 = (i >= j).astype(np.float32)
    b01 = (i <= j).astype(np.float32)
    f01 = np.zeros((128, 128), np.float32)
    f01[:64] = a01[64:]
    c["c_ma"] = np.concatenate([a01, b01], 1).astype(ml_dtypes.bfloat16)
    c["c_m1"] = np.concatenate([f01, b01], 1).astype(ml_dtypes.bfloat16)
    pos = np.arange(smax, dtype=np.float32)
    inv = (np.float32(500000.0) ** (-np.arange(0, 16, 2, dtype=np.float32) / np.float32(16))).astype(np.float32)
    ang = (pos[:, None] * inv[None, :]).astype(np.float32)
    c["c_ropeA"] = np.concatenate([np.cos(ang), np.sin(ang)], 1).astype(np.float32)
    inv2 = (np.float32(10000.0) ** (-np.arange(0, 32, 2, dtype=np.float32) / np.float32(32))).astype(np.float32)
    row = (np.arange(smax) // 64).astype(np.float32)
    col = (np.arange(smax) % 64).astype(np.float32)
    ar = (row[:, None] * inv2[None, :]).astype(np.float32)
    ac = (col[:, None] * inv2[None, :]).astype(np.float32)
    c["c_ropeC"] = np.concatenate([np.cos(ar), np.cos(ac), np.sin(ar), np.sin(ac)], 1).astype(np.float32)
    return c


WSHAPES = {
    "norm_mix_g": (DEPTH, D), "w_in": (DEPTH, D, INW), "mlstm_conv_w": (DEPTH, 3, 512), "mlstm_conv_b": (DEPTH, 512),
    "mlstm_igate_b": (DEPTH, 2, 4), "mlstm_fgate_b": (DEPTH, 2, 4), "mlstm_norm_g": (DEPTH, 256),
    "qk_norm_g": (DEPTH, 2, 64), "w_out": (DEPTH, D, D), "norm_x_g": (DEPTH, D), "norm_mem_g": (DEPTH, D),
    "w_xq": (DEPTH, D, 256), "w_xkv": (DEPTH, D, 512), "w_xo": (DEPTH, 256, D), "norm_ffn_g": (DEPTH, D),
    "w_ffn_up": (DEPTH, D, 2 * DFF), "ffn_conv_w": (DEPTH, 3, DFF), "ffn_conv_b": (DEPTH, DFF),
    "w_ffn_down": (DEPTH, DFF, D), "final_norm_g": (D,),
}
CSHAPES = {"c_ident": ((128, 128), BF), "c_mf": ((128, 128), F32), "c_mb": ((128, 128), F32),
           "c_ma": ((128, 256), BF), "c_m1": ((128, 256), BF),
           "c_ropeA": ((4096, 16), F32), "c_ropeC": ((4096, 64), F32)}


def build_program(seq_lens, depth=DEPTH, dbg=False, phases=("mix", "mixout", "cross", "ffn"), groups=("C", "A", "B")):
    nc = bass.Bass("TRN2", target_bir_lowering=False)
    P = Prog(nc)
    sb = SBAlloc(nc)
    NS = len(seq_lens)
    TOT = sum(seq_lens)
    SOFF = [sum(seq_lens[:i]) for i in range(NS)]
    SMAX = max(seq_lens)

    def din(name, shape, dt=F32):
        return nc.dram_tensor(name, list(shape), dt, kind="ExternalInput").ap()

    xs = [din("xs%d" % i, (S, D)) for i, S in enumerate(seq_lens)]
    msrc = [din("ms%d" % i, (NMEM, D)) for i in range(NS)]
    ys = [nc.dram_tensor("ys%d" % i, [S, D], F32, kind="ExternalOutput").ap() for i, S in enumerate(seq_lens)]
    Wd = {n: din(n, s) for n, s in WSHAPES.items()}
    Cd = {n: din(n, s, dt) for n, (s, dt) in CSHAPES.items()}
    skind = "ExternalOutput" if dbg else "Internal"
    XA = nc.dram_tensor("XA", [TOT, D], F32, kind=skind).ap()
    XB = nc.dram_tensor("XB", [TOT, D], F32, kind=skind).ap()
    XC = nc.dram_tensor("XC", [TOT, D], F32, kind=skind).ap()
    YT = nc.dram_tensor("YT", [D, TOT], BF, kind=skind).ap()
    ZAV = nc.dram_tensor("ZAV", [TOT, 256], BF, kind="Internal").ap()
    XAt = [Tk() for _ in range(TOT // 128)]
    XBt = [Tk() for _ in range(TOT // 128)]
    XCt = [Tk() for _ in range(TOT // 128)]
    YTt = [[Tk() for _ in range(TOT // 512)] for _ in range(16)]
    ZAVt = [Tk() for _ in range(TOT // 128)]
    NOTK = []

    PD = [nc.alloc_psum_tensor("pd%d" % i, [128, 2, 512], F32) for i in range(4)]
    PBK = [Tk("bank%d" % i) for i in range(8)]

    def bank(i):
        return PD[i // 2][:, i % 2, :]

    def bank_bf(i):
        return PD[i // 2][:].bitcast(BF)[:, i % 2, :]

    def cload(name, shape, dt, src, q="sp"):
        t = sb.alloc(shape, dt, name)
        tk = Tk(name)
        P.dma(q, t[:], src, w=[tk])
        return t, tk

    ident, identk = cload("ident", [128, 128], BF, Cd["c_ident"])
    mf32, mf32k = cload("mf32", [128, 128], F32, Cd["c_mf"])
    mb32, mb32k = cload("mb32", [128, 128], F32, Cd["c_mb"])
    maA, maAk = cload("maA", [128, 2, 128], BF, Cd["c_ma"].rearrange("p (a q) -> p a q", a=2))
    m1A, m1Ak = cload("m1A", [128, 2, 128], BF, Cd["c_m1"].rearrange("p (a q) -> p a q", a=2))
    hmask = sb.alloc([128, 4, 65], F32, "hmask")
    hmaskk = Tk()
    P.op("pool", lambda e: e.memset(hmask[:], 0.0), w=[hmaskk])
    P.op("pool", lambda e: e.memset(hmask[0:64].rearrange("p (a e) c -> p a e c", e=2)[:, :, 0, :], 1.0), w=[hmaskk])
    P.op("pool", lambda e: e.memset(hmask[64:128].rearrange("p (a e) c -> p a e c", e=2)[:, :, 1, :], 1.0), w=[hmaskk])
    NTMAX = SMAX // 128
    ones32 = sb.alloc([128, 128], F32, "ones32")
    ones32k = Tk()
    P.op("dve", lambda e: e.memset(ones32[:], 1.0), w=[ones32k])
    cst = sb.alloc([128, 8], F32, "cst")
    cstk = Tk()
    P.op("dve", lambda e: e.memset(cst[:, 0:1], EPS), w=[cstk])
    P.op("dve", lambda e: e.memset(cst[:, 1:2], -0.5), w=[cstk])
    P.op("dve", lambda e: e.memset(cst[:, 2:3], float(np.log(0.125))), w=[cstk])
    P.op("dve", lambda e: e.memset(cst[:, 3:4], 0.0), w=[cstk])

    stat = Ring([(sb.alloc([128, 32], F32, "stat"), Tk()) for _ in range(4)])
    xring = Ring([(sb.alloc([128, D], F32, "xt"), Tk()) for _ in range(3)])
    junk = sb.alloc([128, D], BF, "junk")
    junkk = Tk()
    hring = Ring([(sb.alloc([128, D], BF, "hb"), Tk()) for _ in range(2)])
    gbuf_box = [None]

    def interleaved(n, body, width=2):
        for i0 in range(0, n, width):
            lists = [P.capture(lambda i=i: body(i)) for i in range(i0, min(n, i0 + width))]
            P.emit_interleaved(lists)

    def load_gain(vec_ap):
        g, gk = gbuf_box[0].next()
        P.dma("sp", g[:], vec_ap.partition_broadcast(128), w=[gk])
        return g, gk

    def rstd_from_ss(st, stk, ncol, inv_n):
        P.op("dve", lambda e: e.tensor_scalar(out=st[:, 2 * ncol:3 * ncol], in0=st[:, 0:ncol], scalar1=inv_n, scalar2=EPS,
                                               op0=ALU.mult, op1=ALU.add), r=[stk], w=[stk])
        P.op("pool", lambda e: e.tensor_tensor(out=st[:, ncol:2 * ncol], in0=st[:, 2 * ncol:3 * ncol],
                                                in1=cst[:, 1:2].broadcast_to([128, ncol]), op=ALU.pow),
             r=[stk, cstk], w=[stk])

    def norm_rows(xt, xtk, g, gk, rows=128):
        st, stk = stat.next()
        P.op("dve", lambda e: e.scalar_tensor_tensor(out=junk[0:rows, :], in0=xt[0:rows, :], scalar=1.0, in1=xt[0:rows, :],
                                                      op0=ALU.mult, op1=ALU.mult, accum_out=st[0:rows, 0:1]),
             r=[xtk], w=[junkk, stk])
        rstd_from_ss(st, stk, 1, 1.0 / D)
        h, hk = hring.next()
        P.op("dve", lambda e: e.scalar_tensor_tensor(out=h[0:rows, :], in0=xt[0:rows, :], scalar=st[0:rows, 1:2], in1=g[0:rows, :],
                                                      op0=ALU.mult, op1=ALU.mult), r=[xtk, stk, gk], w=[hk])
        return h, hk

    tb = [6, 7]
    tbi = [0]

    def transpose_to(h, hk, nblk, dst_fn, dstk, rows=128):
        b = tb[tbi[0] % 2]
        tbi[0] += 1
        pv = bank_bf(b)[:, 0:nblk * 128].rearrange("p (k t) -> p k t", k=nblk)

        def fn(e):
            for k in range(nblk):
                ins = e.transpose(out=pv[:, k, 0:rows], in_=h[0:rows, k * 128:(k + 1) * 128], identity=ident[0:rows, 0:rows])
            return ins
        P.op("pe", fn, r=[hk, identk], w=[PBK[b]])
        P.op("act", lambda e: e.copy(out=dst_fn(), in_=pv[:, :, 0:rows]), r=[PBK[b]], w=dstk)

    def build_hT(src, srctk, row0, S, gvec, hT, hTk):
        g, gk = load_gain(gvec)
        P.op("pool", lambda e: e.memset(hT[:, :, 0:1], 0.0), w=[hTk[0]])
        P.op("pool", lambda e: e.memset(hT[:, :, S + 1:S + 2], 0.0), w=[hTk[-1]])
        def body(i):
            xt, xtk = xring.next()
            P.dma("sp", xt[:], src[row0 + i * 128: row0 + (i + 1) * 128, :], r=[srctk[(row0 // 128) + i]] if srctk else [], w=[xtk])
            h, hk = norm_rows(xt, xtk, g, gk)
            transpose_to(h, hk, 8, lambda i=i: hT[:, :, 1 + i * 128: 1 + (i + 1) * 128], [hTk[i]])
        interleaved(S // 128, body)

    NRS = 8
    RS = nc.dram_tensor("RS", [NRS, 2, 512], F32, kind="Internal").ap()
    RSk = [Tk() for _ in range(NRS)]
    rs_i = [0]

    def norm_OT_a(ob, nq):
        osb, osbk = aux["osb"].next()
        P.op("act", lambda e: e.copy(out=osb[0:65, 0:nq], in_=bank(ob)[0:65, 0:nq]), r=[PBK[ob]], w=[osbk])
        P.op("dve", lambda e: e.reciprocal(out=osb[64:65, 0:nq], in_=osb[64:65, 0:nq]), r=[osbk], w=[osbk])
        j = rs_i[0] % NRS
        rs_i[0] += 1
        P.dma("sp", RS[j, 0:1, 0:nq], osb[64:65, 0:nq], r=[osbk], w=[RSk[j]])
        rc, rck = aux["recb"].next()
        P.dma("sp", rc[0:64, 0:nq], RS[j, 0, 0:nq].partition_broadcast(64), r=[RSk[j]], w=[rck])
        return osb, osbk, rc, rck

    def norm_OT_b(st, nq, dst, dstk):
        osb, osbk, rc, rck = st
        P.op("dve", lambda e: e.tensor_tensor(out=dst, in0=osb[0:64, 0:nq], in1=rc[0:64, 0:nq], op=ALU.mult),
             r=[osbk, rck], w=dstk)

    aux = {}

    def alloc_aux(ngain, need_norm=True):
        gbuf_box[0] = Ring([(sb.alloc([128, D], F32, "gb"), Tk()) for _ in range(ngain)])
        if need_norm:
            aux["osb"] = Ring([(sb.alloc([65, 512], F32, "osb"), Tk()) for _ in range(3)])
            aux["recb"] = Ring([(sb.alloc([64, 512], F32, "recb"), Tk()) for _ in range(3)])
            aux["dent"] = Ring([(sb.alloc([128, 8], F32, "dent"), Tk()) for _ in range(4)])

    def xbuf_of(l, stage):
        if stage == 0:
            if l == 0:
                return None
            return XC, XCt
        return (XA, XAt) if stage == 1 else (XB, XBt)

    def mixers(l, si):
        S = seq_lens[si]
        NT = S // 128
        r0 = SOFF[si]
        t0g = r0 // 128
        c0g = r0 // 512
        m0 = sb.mark()
        alloc_aux(1)
        ropeA, ropeAk = cload("ropeA", [128, NT, 16], F32, Cd["c_ropeA"][0:S].rearrange("(t p) c -> p t c", p=128))
        ropeC, ropeCk = cload("ropeC", [128, NT, 64], F32, Cd["c_ropeC"][0:S].rearrange("(t p) c -> p t c", p=128))
        if l == 0:
            src, srctk, srow0 = xs[si], None, 0
        else:
            src, srctk, srow0 = XC, XCt, r0
        mh = sb.mark()
        hT = sb.alloc([128, 8, S + 2], BF, "hT")
        hTk = [Tk() for _ in range(NT)]
        build_hT(src, srctk, srow0, S, Wd["norm_mix_g"][l], hT, hTk)

        def new_wg():
            return sb.alloc([128, 8, 1040], BF, "WG"), Tk()

        def load_wg(WG, WGk, c0, ncol):
            P.dma("pool", WG[:, :, 0:ncol], Wd["w_in"][l][:, c0:c0 + ncol].rearrange("(k p) n -> p k n", p=128), w=[WGk])

        def proj_tok(WG, i, c0, ncol, b, off=0):
            def fn(e):
                for k in range(8):
                    ins = e.matmul(bank(b)[:, off:off + ncol], lhsT=hT[:, k, 1 + i * 128:1 + (i + 1) * 128],
                                   rhs=WG[:, k, c0:c0 + ncol], start=(k == 0), stop=(k == 7))
                return ins
            return fn

        m1 = sb.mark()
        if "C" in groups:
            CQK = sb.alloc([128, 6, S], BF, "CQK")
            CQKk = [Tk() for _ in range(NT)]
            CV = sb.alloc([128, NT, 2, 65], BF, "CV")
            CVk = [Tk() for _ in range(NT)]
            mC = sb.mark()
            WG, WGk = new_wg()
            load_wg(WG, WGk, 1808, 768)
            gtab = sb.alloc([128, 640], F32, "gtab")
            gtabk = Tk()
            gq = Wd["qk_norm_g"][l]
            gtmp = sb.alloc([128, 128], F32, "gtmp")
            gtmpk = Tk()
            P.dma("sp", gtmp[:], gq.rearrange("a d -> (a d)").partition_broadcast(128), w=[gtmpk])
            P.op("dve", lambda e: e.tensor_scalar(out=gtab[:, 0:512].rearrange("p (h d) -> p h d", h=8),
                                                   in0=gtmp[:, 0:64].unsqueeze(1).broadcast_to([128, 8, 64]),
                                                   scalar1=0.125, scalar2=None, op0=ALU.mult), r=[gtmpk], w=[gtabk])
            P.op("dve", lambda e: e.tensor_copy(out=gtab[:, 512:640].rearrange("p (h d) -> p h d", h=2),
                                                 in_=gtmp[:, 64:128].unsqueeze(1).broadcast_to([128, 2, 64])), r=[gtmpk], w=[gtabk])
            P.op("pool", lambda e: e.memset(CV[:, :, :, 64:65], 1.0), w=CVk)
            zc_r = Ring([(sb.alloc([128, 640], F32, "zc"), Tk()) for _ in range(2)])
            sq_r = Ring([(sb.alloc([128, 640], F32, "sq"), Tk()) for _ in range(2)])
            ra_r = Ring([(sb.alloc([128, 320], F32, "ra"), Tk()) for _ in range(2)])
            rb_r = Ring([(sb.alloc([128, 320], F32, "rb"), Tk()) for _ in range(2)])
            rot_r = Ring([(sb.alloc([128, 768], BF, "rot"), Tk()) for _ in range(2)])
            def bodyC(i):
                b0 = (i % 3) * 2
                P.op("pe", proj_tok(WG, i, 0, 512, b0), r=[hTk[i], WGk], w=[PBK[b0]])
                P.op("pe", proj_tok(WG, i, 512, 256, b0 + 1), r=[hTk[i], WGk], w=[PBK[b0 + 1]])
                zc, zck = zc_r.next()
                P.op("act", lambda e, zc=zc, b0=b0: e.copy(out=zc[:, 0:512], in_=bank(b0)[:, 0:512]), r=[PBK[b0]], w=[zck])
                P.op("act", lambda e, zc=zc, b0=b0: e.copy(out=zc[:, 512:640], in_=bank(b0 + 1)[:, 0:128]), r=[PBK[b0 + 1]], w=[zck])
                P.op("act", lambda e, i=i, b0=b0: e.copy(out=CV[:, i, :, 0:64], in_=bank(b0 + 1)[:, 128:256].rearrange("p (g d) -> p g d", g=2)),
                     r=[PBK[b0 + 1]], w=[CVk[i]])
                sq, sqk = sq_r.next()
                st, stk = stat.next()
                P.op("dve", lambda e, zc=zc, sq=sq: e.tensor_tensor(out=sq[:], in0=zc[:], in1=zc[:], op=ALU.mult), r=[zck], w=[sqk])
                P.op("dve", lambda e, sq=sq, st=st: e.tensor_reduce(out=st[:, 0:10], in_=sq[:].rearrange("p (h d) -> p h d", h=10),
                                                                    axis=AX.X, op=ALU.add), r=[sqk], w=[stk])
                rstd_from_ss(st, stk, 10, 1.0 / 64)
                P.op("dve", lambda e, zc=zc, sq=sq, st=st: e.tensor_tensor(
                    out=sq[:].rearrange("p (h d) -> p h d", h=10), in0=zc[:].rearrange("p (h d) -> p h d", h=10),
                    in1=st[:, 10:20].unsqueeze(2).broadcast_to([128, 10, 64]), op=ALU.mult), r=[zck, stk], w=[sqk])
                P.op("dve", lambda e, zc=zc, sq=sq: e.tensor_tensor(out=zc[:], in0=sq[:], in1=gtab[:], op=ALU.mult), r=[sqk, gtabk], w=[zck])
                zv = zc[:].rearrange("p (h a b c) -> p h a b c", h=10, a=2, b=2)
                x1 = zv[:, :, :, 0, :]
                x2 = zv[:, :, :, 1, :]
                cosT = ropeC[:, i, 0:32].rearrange("p (a c) -> p a c", a=2).unsqueeze(1).broadcast_to([128, 10, 2, 16])
                sinT = ropeC[:, i, 32:64].rearrange("p (a c) -> p a c", a=2).unsqueeze(1).broadcast_to([128, 10, 2, 16])
                ra, rak = ra_r.next()
                rb, rbk = rb_r.next()
                rav = ra[:].rearrange("p (h a c) -> p h a c", h=10, a=2)
                rbv = rb[:].rearrange("p (h a c) -> p h a c", h=10, a=2)
                rot, rotk = rot_r.next()
                rv = rot[:, 0:640].rearrange("p (h a b c) -> p h a b c", h=10, a=2, b=2)
                P.op("dve", lambda e, x1=x1, cosT=cosT, rav=rav: e.tensor_tensor(out=rav, in0=x1, in1=cosT, op=ALU.mult), r=[zck, ropeCk], w=[rak])
                P.op("dve", lambda e, x2=x2, sinT=sinT, rbv=rbv: e.tensor_tensor(out=rbv, in0=x2, in1=sinT, op=ALU.mult), r=[zck, ropeCk], w=[rbk])
                P.op("dve", lambda e, rav=rav, rbv=rbv, rv=rv: e.tensor_tensor(out=rv[:, :, :, 0, :], in0=rav, in1=rbv, op=ALU.subtract), r=[rak, rbk], w=[rotk])
                P.op("dve", lambda e, x1=x1, sinT=sinT, rav=rav: e.tensor_tensor(out=rav, in0=x1, in1=sinT, op=ALU.mult), r=[zck, ropeCk], w=[rak])
                P.op("dve", lambda e, x2=x2, cosT=cosT, rbv=rbv: e.tensor_tensor(out=rbv, in0=x2, in1=cosT, op=ALU.mult), r=[zck, ropeCk], w=[rbk])
                P.op("dve", lambda e, rav=rav, rbv=rbv, rv=rv: e.tensor_tensor(out=rv[:, :, :, 1, :], in0=rav, in1=rbv, op=ALU.add), r=[rak, rbk], w=[rotk])
                P.op("pool", lambda e, rot=rot: e.tensor_copy(out=rot[:, 640:768].rearrange("p (a d) -> p a d", a=2),
                                                               in_=rot[:, 576:640].unsqueeze(1).broadcast_to([128, 2, 64])), r=[rotk], w=[rotk])
                P.op("pool", lambda e, rot=rot: e.tensor_copy(out=rot[:, 576:640], in_=rot[:, 512:576]), r=[rotk], w=[rotk])
                transpose_to(rot, rotk, 6, lambda i=i: CQK[:, :, i * 128:(i + 1) * 128], [CQKk[i]])
            interleaved(NT, bodyC)
            P.barrier()
            sb.release(mC)
            pbuf = Ring([(sb.alloc([128, 2, 512], BF, "pbuf"), Tk()) for _ in range(4)])
            ybuf = Ring([(sb.alloc([64, 512], BF, "ybuf"), Tk()) for _ in range(4)])
            for g in range(2):
                for hp2 in range(2):
                    pi = 2 * g + hp2
                    for qc in range(S // 512):
                        ob = [6, 7]
                        qcols = slice(qc * 512, (qc + 1) * 512)
                        qtk = CQKk[qc * 4:(qc + 1) * 4]

                        def s_mm(kt, g=g, pi=pi, qcols=qcols, qtk=qtk):
                            sj = kt % 3

                            def fn(e):
                                for ee in range(2):
                                    ins = e.matmul(PD[sj][:, ee, :], lhsT=CQK[64 * ee:64 * ee + 64, 4 + g, kt * 128:(kt + 1) * 128],
                                                   rhs=CQK[64 * ee:64 * ee + 64, pi, qcols], start=True, stop=True)
                                return ins
                            P.op("pe", fn, r=[CQKk[kt]] + qtk, w=[PBK[2 * sj], PBK[2 * sj + 1]])

                        def pv_mm(kt, pb_, pbk_, g=g, ob=ob):
                            def fn(e):
                                for ee in range(2):
                                    ins = e.matmul(bank(ob[ee])[0:65, :], lhsT=CV[:, kt, g, :], rhs=pb_[:, ee, :],
                                                   start=(kt == 0), stop=(kt == NT - 1))
                                return ins
                            P.op("pe", fn, r=[CVk[kt], pbk_], w=[PBK[ob[0]], PBK[ob[1]]])

                        s_mm(0)
                        s_mm(1)
                        for kt in range(NT):
                            sj = kt % 3
                            if kt + 2 < NT:
                                s_mm(kt + 2)
                            pb_, pbk_ = pbuf.next()
                            P.op("act", lambda e, pb_=pb_, sj=sj: e.activation(out=pb_[:], in_=PD[sj][:], func=AF.Exp),
                                 r=[PBK[2 * sj], PBK[2 * sj + 1]], w=[pbk_])
                            pv_mm(kt, pb_, pbk_)
                        sts = [norm_OT_a(ob[ee], 512) for ee in range(2)]
                        for ee in range(2):
                            h = 4 * g + 2 * hp2 + ee
                            yb, ybk = ybuf.next()
                            norm_OT_b(sts[ee], 512, yb[:, :], [ybk])
                            rg = 8 + h
                            P.dma("sp", YT[512 + 64 * h: 512 + 64 * h + 64, r0 + qc * 512: r0 + (qc + 1) * 512], yb[:, :],
                                  r=[ybk], w=[YTt[rg][c0g + qc]])
            P.barrier()
            sb.release(m1)

        if "A" in groups:
            AQK = sb.alloc([128, 4, S], BF, "AQK")
            AQKk = Tk()
            mA = sb.mark()
            WG, WGk = new_wg()
            load_wg(WG, WGk, 0, 768)
            zc_r = Ring([(sb.alloc([128, 512], F32, "zcA"), Tk()) for _ in range(2)])
            ta_r = Ring([(sb.alloc([128, 4, 64], F32, "taA"), Tk()) for _ in range(2)])
            zb_r = Ring([(sb.alloc([128, 512], BF, "zbA"), Tk()) for _ in range(2)])
            vt_r = Ring([(sb.alloc([128, 256], BF, "vtA"), Tk()) for _ in range(2)])
            def bodyA(i):
                b0 = (i % 3) * 2
                P.op("pe", proj_tok(WG, i, 0, 512, b0), r=[hTk[i], WGk], w=[PBK[b0]])
                P.op("pe", proj_tok(WG, i, 512, 256, b0 + 1), r=[hTk[i], WGk], w=[PBK[b0 + 1]])
                zc, zck = zc_r.next()
                P.op("act", lambda e, zc=zc, b0=b0: e.copy(out=zc[:], in_=bank(b0)[:, :]), r=[PBK[b0]], w=[zck])
                vt, vtk = vt_r.next()
                P.op("act", lambda e, vt=vt, b0=b0: e.copy(out=vt[:], in_=bank(b0 + 1)[:, 0:256]), r=[PBK[b0 + 1]], w=[vtk])
                P.dma("sp", ZAV[r0 + i * 128: r0 + (i + 1) * 128, :], vt[:], r=[vtk], w=[ZAVt[t0g + i]])
                zv = zc[:].rearrange("p (h d) -> p h d", h=8)
                x1 = zv[:, :, 0:8]
                x2 = zv[:, :, 8:16]
                cosT = ropeA[:, i, 0:8].unsqueeze(1).broadcast_to([128, 8, 8])
                sinT = ropeA[:, i, 8:16].unsqueeze(1).broadcast_to([128, 8, 8])
                ta, tak = ta_r.next()
                tv = ta[:].rearrange("p a (h c) -> p a h c", h=8)
                P.op("dve", lambda e, x1=x1, cosT=cosT, tv=tv: e.tensor_tensor(out=tv[:, 0], in0=x1, in1=cosT, op=ALU.mult), r=[zck, ropeAk], w=[tak])
                P.op("dve", lambda e, x2=x2, sinT=sinT, tv=tv: e.tensor_tensor(out=tv[:, 1], in0=x2, in1=sinT, op=ALU.mult), r=[zck, ropeAk], w=[tak])
                P.op("dve", lambda e, x1=x1, sinT=sinT, tv=tv: e.tensor_tensor(out=tv[:, 2], in0=x1, in1=sinT, op=ALU.mult), r=[zck, ropeAk], w=[tak])
                P.op("dve", lambda e, x2=x2, cosT=cosT, tv=tv: e.tensor_tensor(out=tv[:, 3], in0=x2, in1=cosT, op=ALU.mult), r=[zck, ropeAk], w=[tak])
                P.op("dve", lambda e, x1=x1, tv=tv: e.tensor_tensor(out=x1, in0=tv[:, 0], in1=tv[:, 1], op=ALU.subtract), r=[tak], w=[zck])
                P.op("dve", lambda e, x2=x2, tv=tv: e.tensor_tensor(out=x2, in0=tv[:, 2], in1=tv[:, 3], op=ALU.add), r=[tak], w=[zck])
                zb, zbk = zb_r.next()
                P.op("pool", lambda e, zb=zb, zc=zc: e.tensor_copy(out=zb[:], in_=zc[:]), r=[zck], w=[zbk])
                transpose_to(zb, zbk, 4, lambda i=i: AQK[:, :, i * 128:(i + 1) * 128], [AQKk])
            interleaved(NT, bodyA)
            P.barrier()
            sb.release(mA)
            OACC = sb.alloc([65, 2, S], F32, "OACC")
            OACCk = Tk()
            BRS = ((128, 1), (512, 4), (2048, 16))
            NTV = max(d * (S // d // 128 + 1) for (_, d) in BRS)
            VDs = []
            for _vi in range(2):
                VD_ = sb.alloc([128, NTV, 2, 65], BF, "VD")
                VDk_ = Tk()
                P.op("pool", lambda e, VD_=VD_: e.memset(VD_[:, :, :, 64:65], 1.0), w=[VDk_])
                VDs.append((VD_, VDk_))
            pA = Ring([(sb.alloc([128, 2, 2, 128], BF, "pA"), Tk()) for _ in range(3)])
            ybuf = Ring([(sb.alloc([64, 512], BF, "ybufA"), Tk()) for _ in range(2)])
            zsrc = ZAV[r0:r0 + S, :]
            ztk = ZAVt[t0g:t0g + NT]
            units = [(hp, d) for hp in range(2) for (_, d) in BRS]

            def load_unit(u):
                hp, d = units[u]
                VD, VDk = VDs[u % 2]
                L = S // d
                nb = L // 128
                zv = zsrc.rearrange("(j d) c -> d j c", d=d)
                for r in range(d):
                    tb0 = r * (nb + 1)
                    P.dma("sp", VD[0:64, tb0, :, 0:64], zv[r, 0:64, hp * 128:(hp + 1) * 128].rearrange("j (h c) -> j h c", h=2),
                          r=ztk, w=[VDk])
                    P.dma("sp", VD[0:64, tb0 + nb, :, 0:64], zv[r, L - 64:L, hp * 128:(hp + 1) * 128].rearrange("j (h c) -> j h c", h=2),
                          r=ztk, w=[VDk])
                    for m in range(1, nb):
                        P.dma("sp", VD[:, tb0 + m, :, 0:64],
                              zv[r, 128 * m - 64:128 * m + 64, hp * 128:(hp + 1) * 128].rearrange("j (h c) -> j h c", h=2),
                              r=ztk, w=[VDk])

            load_unit(0)
            for u, (hp, d) in enumerate(units):
                if u + 1 < len(units):
                    load_unit(u + 1)
                VD, VDk = VDs[u % 2]
                if u % 3 == 0:
                    P.op("pool", lambda e: e.memset(OACC[:], 0.0), w=[OACCk])
                L = S // d
                nb = L // 128
                blocks = []
                for r in range(d):
                    tb0 = r * (nb + 1)
                    for blk in range(nb):
                        j0 = 128 * blk
                        qsl = ssl(j0 * d + r, 128, d)
                        if blk == 0:
                            Ka, ksa, mka, mkak = 64, ssl(r, 64, d), m1A, m1Ak
                        else:
                            Ka, ksa, mka, mkak = 128, ssl((j0 - 64) * d + r, 128, d), maA, maAk
                        if blk == nb - 1:
                            Kb, ksb = 64, ssl((j0 + 64) * d + r, 64, d)
                        else:
                            Kb, ksb = 128, ssl((j0 + 64) * d + r, 128, d)
                        blocks.append((qsl, Ka, ksa, mka, mkak, Kb, ksb, tb0 + blk, tb0 + blk + 1))
                nblk = len(blocks)

                def emit_S(bi, hp=hp, blocks=blocks):
                    qsl, Ka, ksa, mka, mkak, Kb, ksb, ta_, tb_ = blocks[bi]
                    sj = bi % 3
                    sview = PD[sj][:, :, 0:256].rearrange("p e (a q) -> p e a q", a=2)

                    def fn(e):
                        for ee in range(2):
                            pbs = slice(64 * ee, 64 * ee + 64)
                            e.matmul(sview[0:Ka, ee, 0, :], lhsT=AQK[pbs, 2 + hp, ksa], rhs=AQK[pbs, hp, qsl], start=True, stop=True)
                            ins = e.matmul(sview[0:Kb, ee, 1, :], lhsT=AQK[pbs, 2 + hp, ksb], rhs=AQK[pbs, hp, qsl], start=True, stop=True)
                        return ins
                    P.op("pe", fn, r=[AQKk], w=[PBK[2 * sj], PBK[2 * sj + 1]])

                pas = {}

                def emit_p(bi, blocks=blocks, pas=pas):
                    qsl, Ka, ksa, mka, mkak, Kb, ksb, ta_, tb_ = blocks[bi]
                    sj = bi % 3
                    sview = PD[sj][:, :, 0:256].rearrange("p e (a q) -> p e a q", a=2)
                    sbks = [PBK[2 * sj], PBK[2 * sj + 1]]
                    pa, pak = pA.next()
                    pas[bi] = (pa, pak)
                    P.op("act", lambda e: e.activation(out=pa[:], in_=sview, func=AF.Exp, scale=0.125), r=sbks, w=[pak])
                    P.op("dve", lambda e: e.tensor_tensor(out=pa[:], in0=pa[:], in1=mka[:].unsqueeze(1).broadcast_to([128, 2, 2, 128]),
                                                          op=ALU.mult), r=[pak, mkak], w=[pak])

                def emit_o(bi, blocks=blocks, VD=VD, VDk=VDk, pas=pas):
                    qsl, Ka, ksa, mka, mkak, Kb, ksb, ta_, tb_ = blocks[bi]
                    obk = 6 + bi % 2
                    pa, pak = pas.pop(bi)
                    oview = bank(obk)[:, 0:256].rearrange("p (e q) -> p e q", e=2)

                    def fn2(e):
                        for ee in range(2):
                            e.matmul(oview[0:65, ee, :], lhsT=VD[0:Ka, ta_, ee, :], rhs=pa[0:Ka, ee, 0, :], start=True, stop=False)
                            ins = e.matmul(oview[0:65, ee, :], lhsT=VD[0:Kb, tb_, ee, :], rhs=pa[0:Kb, ee, 1, :], start=False, stop=True)
                        return ins
                    P.op("pe", fn2, r=[VDk, pak], w=[PBK[obk]])
                    P.op("dve", lambda e: e.tensor_tensor(out=OACC[0:65, :, qsl], in0=OACC[0:65, :, qsl], in1=oview[0:65, :, :], op=ALU.add),
                         r=[PBK[obk], OACCk], w=[OACCk])

                emit_S(0)
                if nblk > 1:
                    emit_S(1)
                emit_p(0)
                for bi in range(nblk):
                    if bi + 2 < nblk:
                        emit_S(bi + 2)
                    if bi + 1 < nblk:
                        emit_p(bi + 1)
                    emit_o(bi)
                if u % 3 != 2:
                    continue
                for qc in range(S // 512):
                    for ee in range(2):
                        h = 2 * hp + ee
                        bb = 0
                        P.op("pe", lambda e, ee=ee, qc=qc: e.matmul(bank(0)[0:64, :], lhsT=ones32[64:65, 0:64],
                                                                      rhs=OACC[64:65, ee, qc * 512:(qc + 1) * 512], start=True, stop=True),
                             r=[ones32k, OACCk], w=[PBK[bb]])
                        rc, rck = aux["recb"].next()
                        P.op("dve", lambda e, rc=rc: e.reciprocal(out=rc[0:64, :], in_=bank(0)[0:64, :]), r=[PBK[bb]], w=[rck])
                        yb, ybk = ybuf.next()
                        P.op("dve", lambda e, yb=yb, rc=rc, ee=ee, qc=qc: e.tensor_tensor(out=yb[:, :], in0=OACC[0:64, ee, qc * 512:(qc + 1) * 512],
                                                                                          in1=rc[0:64, :], op=ALU.mult), r=[OACCk, rck], w=[ybk])
                        P.dma("sp", YT[64 * h:64 * h + 64, r0 + qc * 512:r0 + (qc + 1) * 512], yb[:, :], r=[ybk], w=[YTt[h][c0g + qc]])
            P.barrier()
            sb.release(m1)

        if "B" in groups:
            l_ = l
            BQK = sb.alloc([128, 4, S], BF, "BQK")
            BQKk = [Tk() for _ in range(S // 256)]
            BVR = sb.alloc([128, NT, 4, 64], BF, "BVR")
            BVRk = [Tk() for _ in range(NT)]
            BO = sb.alloc([128, NT, 256], BF, "BO")
            BOk = [Tk() for _ in range(NT)]
            EB = sb.alloc([128, NT, 8], F32, "EB")
            ET = sb.alloc([128, NT, 8], F32, "ET")
            VSC = sb.alloc([128, NT, 8], F32, "VSC")
            EBk = [Tk() for _ in range(NT)]
            ngt = sb.alloc([128, 256], F32, "ngt")
            ngtk = Tk()
            P.dma("sp", ngt[:], Wd["mlstm_norm_g"][l_].partition_broadcast(128), w=[ngtk])
            mBp = sb.mark()
            WG, WGk = new_wg()
            load_wg(WG, WGk, 768, 1040)
            cw = sb.alloc([128, 4, 4], F32, "cwB")
            cwk = Tk()
            for c in range(4):
                for j in range(3):
                    P.dma("sp", cw[:, c, j:j + 1], Wd["mlstm_conv_w"][l_, j, c * 128:(c + 1) * 128].rearrange("(p o) -> p o", o=1), w=[cwk])
                P.dma("sp", cw[:, c, 3:4], Wd["mlstm_conv_b"][l_, c * 128:(c + 1) * 128].rearrange("(p o) -> p o", o=1), w=[cwk])
            gbias = sb.alloc([128, 16], F32, "gbias")
            gbiask = Tk()
            for dd in range(2):
                P.dma("sp", gbias[:, dd * 8:dd * 8 + 4], Wd["mlstm_igate_b"][l_, dd, :].partition_broadcast(128), w=[gbiask])
                P.dma("sp", gbias[:, dd * 8 + 4:dd * 8 + 8], Wd["mlstm_fgate_b"][l_, dd, :].partition_broadcast(128), w=[gbiask])
            t0_r = Ring([(sb.alloc([128, 256], F32, "t0B"), Tk()) for _ in range(2)])
            t1_r = Ring([(sb.alloc([128, 256], F32, "t1B"), Tk()) for _ in range(2)])
            for sw in range(S // 256):
                def bodyQ(c, sw=sw):
                    b = (sw * 4 + c) % 4

                    def fn(e, c=c, sw=sw, b=b, WG=WG):
                        for k in range(8):
                            ins = e.matmul(bank(b)[:, 0:258], lhsT=WG[:, k, c * 128:(c + 1) * 128], rhs=hT[:, k, sw * 256:sw * 256 + 258],
                                           start=(k == 0), stop=(k == 7))
                        return ins
                    P.op("pe", fn, r=hTk[sw * 2:sw * 2 + 2] + ([hTk[sw * 2 - 1]] if sw > 0 else []) + ([hTk[sw * 2 + 2]] if sw * 2 + 2 < NT else []) + [hTk[0], hTk[-1], WGk],
                         w=[PBK[b]])
                    t0, t0k = t0_r.next()
                    t1, t1k = t1_r.next()
                    P.op("act", lambda e, t0=t0, b=b, c=c: e.activation(out=t0[:], in_=bank(b)[:, 1:257], func=AF.Identity,
                                                                           scale=cw[:, c, 1:2], bias=cw[:, c, 3:4]), r=[PBK[b], cwk], w=[t0k])
                    P.op("dve", lambda e, t0=t0, t1=t1, b=b, c=c: e.scalar_tensor_tensor(out=t1[:], in0=bank(b)[:, 0:256], scalar=cw[:, c, 0:1],
                                                                                        in1=t0[:], op0=ALU.mult, op1=ALU.add), r=[PBK[b], cwk, t0k], w=[t1k])
                    P.op("dve", lambda e, t0=t0, t1=t1, b=b, c=c: e.scalar_tensor_tensor(out=t0[:], in0=bank(b)[:, 2:258], scalar=cw[:, c, 2:3],
                                                                                        in1=t1[:], op0=ALU.mult, op1=ALU.add), r=[PBK[b], cwk, t1k], w=[t0k])
                    P.op("act", lambda e, t0=t0, c=c, sw=sw: e.activation(out=BQK[:, c, sw * 256:(sw + 1) * 256], in_=t0[:], func=AF.Silu),
                         r=[t0k], w=[BQKk[sw]])
                interleaved(4, bodyQ)
            gs_r = Ring([(sb.alloc([128, 64], F32, "gsB"), Tk()) for _ in range(2)])
            def bodyB(i):
                b0 = 4 + (i % 2)
                bq = 6 + (i % 2)
                P.op("pe", proj_tok(WG, i, 512, 512, b0), r=[hTk[i], WGk], w=[PBK[b0]])
                P.op("pe", proj_tok(WG, i, 1024, 16, bq, off=0), r=[hTk[i], WGk], w=[PBK[bq]])
                gs, gsk = gs_r.next()
                P.op("dve", lambda e, gs=gs, bq=bq: e.tensor_tensor(out=gs[:, 0:16], in0=bank(bq)[:, 0:16], in1=gbias[:], op=ALU.add), r=[PBK[bq], gbiask], w=[gsk])
                gv = gs[:, 0:16].rearrange("p (d k h) -> p d k h", d=2, k=2)
                P.op("act", lambda e, gs=gs, gv=gv: e.activation(out=gs[:, 16:24].rearrange("p (d h) -> p d h", d=2), in_=gv[:, :, 1, :], func=AF.Sigmoid),
                     r=[gsk], w=[gsk])
                P.op("act", lambda e, gs=gs: e.activation(out=gs[:, 16:24], in_=gs[:, 16:24], func=AF.Ln), r=[gsk], w=[gsk])

                def fn(e, gs=gs, bq=bq):
                    e.matmul(bank(bq)[:, 16:20], lhsT=mf32[:], rhs=gs[:, 16:20], start=True, stop=True)
                    e.matmul(bank(bq)[:, 20:24], lhsT=mb32[:], rhs=gs[:, 20:24], start=True, stop=True)
                    return e.matmul(bank(bq)[:, 24:32], lhsT=ones32[:], rhs=gs[:, 16:24], start=True, stop=True)
                P.op("pe", fn, r=[gsk, mf32k, mb32k, ones32k], w=[PBK[bq]])
                P.op("act", lambda e, i=i, bq=bq: e.activation(out=EB[:, i, :], in_=bank(bq)[:, 16:24], func=AF.Exp), r=[PBK[bq]], w=[EBk[i]])
                P.op("act", lambda e, i=i, bq=bq: e.activation(out=ET[:, i, :], in_=bank(bq)[:, 24:32], func=AF.Exp), r=[PBK[bq]], w=[EBk[i]])
                P.op("dve", lambda e, gs=gs, gv=gv, bq=bq: e.tensor_tensor(out=gs[:, 24:32].rearrange("p (d h) -> p d h", d=2), in0=gv[:, :, 0, :],
                                                                     in1=bank(bq)[:, 16:24].rearrange("p (d h) -> p d h", d=2), op=ALU.subtract),
                     r=[gsk, PBK[bq]], w=[gsk])
                P.op("act", lambda e, gs=gs, i=i: e.activation(out=VSC[:, i, :], in_=gs[:, 24:32], func=AF.Exp, bias=cst[:, 2:3]), r=[gsk, cstk], w=[EBk[i]])
                P.op("act", lambda e, i=i, b0=b0: e.copy(out=BVR[:, i, :, :], in_=bank(b0)[:, 0:256].rearrange("p (h c) -> p h c", h=4)), r=[PBK[b0]], w=[BVRk[i]])
                P.op("act", lambda e, i=i, b0=b0: e.activation(out=BO[:, i, :], in_=bank(b0)[:, 256:512], func=AF.Sigmoid), r=[PBK[b0]], w=[BOk[i]])
            interleaved(NT, bodyB)
            P.barrier()
            sb.release(mBp)
            top = sb.mark()
            sb.release(mh)
            KT = sb.alloc([128, NT, 256], BF, "KT")
            KTk = [Tk() for _ in range(NT)]
            HF = sb.alloc([128, NT, 256], F32, "HF")
            HFk = [Tk() for _ in range(NT)]
            assert sb.off <= m1, "recurrence buffers overflow the hT region"
            sb.off = top
            for i in range(NT):
                b = tb[tbi[0] % 2]
                tbi[0] += 1
                pv = bank_bf(b)[:, 0:256].rearrange("p (k t) -> p k t", k=2)

                def fn(e, pv=pv, i=i):
                    for k in range(2):
                        ins = e.transpose(out=pv[:, k, :], in_=BQK[:, 2 + k, i * 128:(i + 1) * 128], identity=ident[:])
                    return ins
                P.op("pe", fn, r=[BQKk[i // 2], identk], w=[PBK[b]])
                P.op("act", lambda e, pv=pv, i=i: e.copy(out=KT[:, i, :].rearrange("p (k t) -> p k t", k=2), in_=pv), r=[PBK[b]], w=[KTk[i]])
            C32 = sb.alloc([128, 4, 65], F32, "C32")
            Cbfs = [(sb.alloc([128, 4, 65], BF, "Cbf"), Tk()) for _ in range(2)]
            C32k = Tk()
            va_r = Ring([(sb.alloc([128, 4, 65], BF, "vaB"), Tk()) for _ in range(3)])
            st_r = Ring([(sb.alloc([128, 4, 128], BF, "stB"), Tk()) for _ in range(3)])
            tt_r = Ring([(sb.alloc([128, 4, 65], F32, "ttB"), Tk()) for _ in range(2)])
            dn_r = Ring([(sb.alloc([128, 8], F32, "dnB"), Tk()) for _ in range(2)])
            hs_r = Ring([(sb.alloc([128, 256], F32, "hsB"), Tk()) for _ in range(2)])
            h2_r = Ring([(sb.alloc([128, 256], F32, "h2B"), Tk()) for _ in range(2)])
            yo_r = Ring([(sb.alloc([128, 256], BF, "yoB"), Tk()) for _ in range(2)])
            ystage = Ring([(sb.alloc([128, 2, 512], BF, "ystB"), Tk()) for _ in range(2)])
            for dd in range(2):
                order = list(range(NT)) if dd == 0 else list(range(NT - 1, -1, -1))
                msk, mskk = (mf32, mf32k) if dd == 0 else (mb32, mb32k)
                ystbox = [None]
                prepd = {}

                def prep(n, dd=dd, order=order, msk=msk, mskk=mskk, prepd=prepd):
                    i = order[n]
                    tsl = slice(i * 128, (i + 1) * 128)
                    sbk = n % 2
                    va, vak = va_r.next()
                    P.op("dve", lambda e: e.tensor_tensor(out=va[:, :, 0:64], in0=BVR[:, i, :, :],
                                                          in1=VSC[:, i, 4 * dd:4 * dd + 4].unsqueeze(2).broadcast_to([128, 4, 64]), op=ALU.mult),
                         r=[BVRk[i], EBk[i]], w=[vak])
                    P.op("pool", lambda e: e.tensor_copy(out=va[:, :, 64:65], in_=VSC[:, i, 4 * dd:4 * dd + 4].unsqueeze(2)),
                         r=[EBk[i]], w=[vak])

                    def fn(e):
                        for h in range(4):
                            pbs = slice(64 * (h % 2), 64 * (h % 2) + 64)
                            ins = e.matmul(PD[sbk][:, h % 2, (h // 2) * 128:(h // 2 + 1) * 128], lhsT=BQK[pbs, 2 + h // 2, tsl], rhs=BQK[pbs, h // 2, tsl], start=True, stop=True)
                        return ins
                    P.op("pe", fn, r=[BQKk[i // 2]], w=[PBK[2 * sbk], PBK[2 * sbk + 1]])
                    stt, sttk = st_r.next()
                    P.op("dve", lambda e: e.tensor_tensor(out=stt[:].rearrange("p (hp e) l -> p hp e l", hp=2),
                                                          in0=PD[sbk][:, :, 0:256].rearrange("p e (hp l) -> p hp e l", hp=2),
                                                          in1=msk[:].unsqueeze(1).unsqueeze(1).broadcast_to([128, 2, 2, 128]), op=ALU.mult),
                         r=[PBK[2 * sbk], PBK[2 * sbk + 1], mskk], w=[sttk])
                    prepd[n] = (va, vak, stt, sttk)

                def step(n, dd=dd, order=order, prepd=prepd, ystbox=ystbox):
                    i = order[n]
                    tsl = slice(i * 128, (i + 1) * 128)
                    va, vak, stt, sttk = prepd.pop(n)
                    Cprev, Cprevk = Cbfs[(n + 1) % 2]
                    Cnew, Cnewk = Cbfs[n % 2]
                    cbk = 6 + n % 2
                    cv = bank(cbk)[:, 0:260].rearrange("p (h c) -> p h c", h=4)
                    if n < NT - 1:
                        def fn3(e):
                            for h in range(4):
                                ins = e.matmul(cv[:, h, :], lhsT=KT[:, i, (h // 2) * 128:(h // 2 + 1) * 128], rhs=va[:, h, :], start=True, stop=True)
                            return ins
                        P.op("pe", fn3, r=[KTk[i], vak], w=[PBK[cbk]])
                    obk = 4 + n % 2
                    ov = bank(obk)[:, 0:260].rearrange("p (h c) -> p h c", h=4)

                    def fn2(e):
                        for h in range(4):
                            ins = e.matmul(ov[:, h, :], lhsT=stt[:, h, :], rhs=va[:, h, :], start=True, stop=(n == 0))
                            if n > 0:
                                ins = e.matmul(ov[:, h, :], lhsT=BQK[:, h // 2, tsl], rhs=Cprev[:, h, :], start=False, stop=True)
                        return ins
                    P.op("pe", fn2, r=[sttk, vak, BQKk[i // 2]] + ([Cprevk] if n > 0 else []), w=[PBK[obk]])
                    if n < NT - 1:
                        etb = ET[:, i, 4 * dd:4 * dd + 4].unsqueeze(2).broadcast_to([128, 4, 65])
                        if n == 0:
                            P.op("dve", lambda e: e.tensor_tensor(out=C32[:], in0=cv, in1=etb, op=ALU.mult), r=[PBK[cbk], EBk[i]], w=[C32k])
                        else:
                            P.op("dve", lambda e: e.tensor_tensor(out=C32[:], in0=cv, in1=C32[:], op=ALU.add), r=[PBK[cbk], C32k], w=[C32k])
                            P.op("dve", lambda e: e.tensor_tensor(out=C32[:], in0=C32[:], in1=etb, op=ALU.mult), r=[C32k, EBk[i]], w=[C32k])
                        P.op("pool", lambda e: e.tensor_tensor(out=Cnew[:], in0=C32[:], in1=hmask[:], op=ALU.mult), r=[C32k, hmaskk], w=[Cnewk])
                    tt, ttk = tt_r.next()
                    P.op("dve", lambda e: e.tensor_tensor(out=tt[:], in0=ov, in1=EB[:, i, 4 * dd:4 * dd + 4].unsqueeze(2).broadcast_to([128, 4, 65]),
                                                          op=ALU.mult), r=[PBK[obk], EBk[i]], w=[ttk])
                    dn, dnk = dn_r.next()
                    P.op("dve", lambda e: e.scalar_tensor_tensor(out=dn[:, 0:4], in0=tt[:, :, 64], scalar=-1.0, in1=tt[:, :, 64],
                                                                 op0=ALU.mult, op1=ALU.max), r=[ttk], w=[dnk])
                    P.op("dve", lambda e: e.tensor_scalar(out=dn[:, 0:4], in0=dn[:, 0:4], scalar1=1.0, scalar2=None, op0=ALU.max), r=[dnk], w=[dnk])
                    P.op("dve", lambda e: e.reciprocal(out=dn[:, 4:8], in_=dn[:, 0:4]), r=[dnk], w=[dnk])
                    if dd == 0:
                        P.op("dve", lambda e: e.tensor_tensor(out=HF[:, i, :].rearrange("p (h c) -> p h c", h=4), in0=tt[:, :, 0:64],
                                                              in1=dn[:, 4:8].unsqueeze(2).broadcast_to([128, 4, 64]), op=ALU.mult),
                             r=[ttk, dnk], w=[HFk[i]])
                        return
                    hs, hsk = hs_r.next()
                    h2, h2k = h2_r.next()
                    P.op("dve", lambda e: e.tensor_tensor(out=hs[:].rearrange("p (h c) -> p h c", h=4), in0=tt[:, :, 0:64],
                                                          in1=dn[:, 4:8].unsqueeze(2).broadcast_to([128, 4, 64]), op=ALU.mult),
                         r=[ttk, dnk], w=[hsk])
                    P.op("dve", lambda e: e.tensor_tensor(out=hs[:], in0=hs[:], in1=HF[:, i, :], op=ALU.add), r=[hsk, HFk[i]], w=[hsk])
                    st, stk = stat.next()
                    P.op("dve", lambda e: e.tensor_tensor(out=h2[:], in0=hs[:], in1=hs[:], op=ALU.mult), r=[hsk], w=[h2k])
                    P.op("dve", lambda e: e.tensor_reduce(out=st[:, 0:4], in_=h2[:].rearrange("p (h c) -> p h c", h=4), axis=AX.X, op=ALU.add),
                         r=[h2k], w=[stk])
                    rstd_from_ss(st, stk, 4, 1.0 / 64)
                    P.op("dve", lambda e: e.tensor_tensor(out=h2[:].rearrange("p (h c) -> p h c", h=4), in0=hs[:].rearrange("p (h c) -> p h c", h=4),
                                                          in1=st[:, 4:8].unsqueeze(2).broadcast_to([128, 4, 64]), op=ALU.mult),
                         r=[hsk, stk], w=[h2k])
                    P.op("dve", lambda e: e.tensor_tensor(out=hs[:], in0=h2[:], in1=ngt[:], op=ALU.mult), r=[h2k, ngtk], w=[hsk])
                    yo, yok = yo_r.next()
                    P.op("dve", lambda e: e.tensor_tensor(out=yo[:], in0=hs[:], in1=BO[:, i, :], op=ALU.mult), r=[hsk, BOk[i]], w=[yok])
                    if n % 4 == 0:
                        ystbox[0] = ystage.next()
                    ysb, ysk = ystbox[0]
                    j = i % 4
                    transpose_to(yo, yok, 2, lambda: ysb[:, :, j * 128:(j + 1) * 128], [ysk])
                    if n % 4 == 3:
                        qc = i // 4
                        for k in range(2):
                            P.dma("sp", YT[256 + 128 * k:256 + 128 * (k + 1), r0 + qc * 512:r0 + (qc + 1) * 512], ysb[:, k, :], r=[ysk],
                                  w=[YTt[4 + 2 * k][c0g + qc], YTt[5 + 2 * k][c0g + qc]])

                prep(0)
                for n in range(NT):
                    lists = []
                    if n + 1 < NT:
                        lists.append(P.capture(lambda n=n: prep(n + 1)))
                    lists.insert(0, P.capture(lambda n=n: step(n)))
                    P.emit_interleaved(lists)
        P.barrier()
        sb.release(m0)

    def mixer_out(l, si, w_out_sb, w_out_k):
        S = seq_lens[si]
        r0 = SOFF[si]
        c0g = r0 // 512
        m0 = sb.mark()
        yts = Ring([(sb.alloc([128, 8, 512], BF, "yts"), Tk()) for _ in range(2)])
        xo_r = Ring([(sb.alloc([128, D], F32, "xo"), Tk()) for _ in range(2)])
        if l == 0:
            src, srctk, srow0 = xs[si], None, 0
        else:
            src, srctk, srow0 = XC, XCt, r0
        for qc in range(S // 512):
            yt, ytk = yts.next()
            P.dma("sp", yt[:], YT[:, r0 + qc * 512:r0 + (qc + 1) * 512].rearrange("(k p) t -> p k t", p=128),
                  r=[YTt[rg][c0g + qc] for rg in range(16)], w=[ytk])
            def bodyM(tt, qc=qc, yt=yt, ytk=ytk):
                ti = qc * 4 + tt
                xt, xtk = xring.next()
                P.dma("sp", xt[:], src[srow0 + ti * 128:srow0 + (ti + 1) * 128, :], r=[srctk[srow0 // 128 + ti]] if srctk else [], w=[xtk])
                xo, xok = xo_r.next()
                for nh in range(2):
                    b = (tt * 2 + nh) % 4
                    def fn(e, yt=yt, tt=tt, nh=nh, b=b):
                        for k in range(8):
                            ins = e.matmul(bank(b)[:, :], lhsT=yt[:, k, tt * 128:(tt + 1) * 128], rhs=w_out_sb[:, k, nh * 512:(nh + 1) * 512],
                                           start=(k == 0), stop=(k == 7))
                        return ins
                    P.op("pe", fn, r=[ytk, w_out_k], w=[PBK[b]])
                    P.op("dve", lambda e, xo=xo, xt=xt, nh=nh, b=b: e.tensor_tensor(out=xo[:, nh * 512:(nh + 1) * 512], in0=bank(b)[:, :],
                                                                                    in1=xt[:, nh * 512:(nh + 1) * 512], op=ALU.add), r=[PBK[b], xtk], w=[xok])
                P.dma("pool", XA[r0 + ti * 128:r0 + (ti + 1) * 128, :], xo[:], r=[xok], w=[XAt[r0 // 128 + ti]])
            interleaved(4, bodyM)
        P.barrier()
        sb.release(m0)

    def cross(l, si, Wq, Wkv, Wo, wk):
        S = seq_lens[si]
        NT = S // 128
        r0 = SOFF[si]
        m0 = sb.mark()
        alloc_aux(2)
        hT = sb.alloc([128, 8, S + 2], BF, "hTx")
        hTk = [Tk() for _ in range(NT)]
        build_hT(XA, XAt, r0, S, Wd["norm_x_g"][l], hT, hTk)
        mT = sb.alloc([128, 8, 256], BF, "mT")
        mTk = Tk()
        g, gk = load_gain(Wd["norm_mem_g"][l])
        for i in range(2):
            xt, xtk = xring.next()
            P.dma("sp", xt[:], msrc[si][i * 128:(i + 1) * 128, :], w=[xtk])
            h, hk = norm_rows(xt, xtk, g, gk)
            transpose_to(h, hk, 8, lambda i=i: mT[:, :, i * 128:(i + 1) * 128], [mTk])
        XKT = sb.alloc([128, 2, 256], BF, "XKT")
        XKTk = Tk()
        XV = sb.alloc([128, 2, 4, 65], BF, "XV")
        XVk = Tk()
        P.op("pool", lambda e: e.memset(XV[:, :, :, 64:65], 1.0), w=[XVk])
        for hp in range(2):
            def fn(e, hp=hp):
                for k in range(8):
                    ins = e.matmul(bank(hp)[:, 0:256], lhsT=Wkv[:, k, hp * 128:(hp + 1) * 128], rhs=mT[:, k, :], start=(k == 0), stop=(k == 7))
                return ins
            P.op("pe", fn, r=[wk, mTk], w=[PBK[hp]])
            P.op("act", lambda e, hp=hp: e.copy(out=XKT[:, hp, :], in_=bank(hp)[:, 0:256]), r=[PBK[hp]], w=[XKTk])
        for mt in range(2):
            def fn(e, mt=mt):
                for k in range(8):
                    ins = e.matmul(bank(2 + mt)[:, 0:256], lhsT=mT[:, k, mt * 128:(mt + 1) * 128], rhs=Wkv[:, k, 256:512], start=(k == 0), stop=(k == 7))
                return ins
            P.op("pe", fn, r=[wk, mTk], w=[PBK[2 + mt]])
            P.op("act", lambda e, mt=mt: e.copy(out=XV[:, mt, :, 0:64], in_=bank(2 + mt)[:, 0:256].rearrange("p (h c) -> p h c", h=4)),
                 r=[PBK[2 + mt]], w=[XVk])
        XQT_r = Ring([(sb.alloc([128, 2, 512], BF, "XQT"), Tk()) for _ in range(2)])
        XO_r = Ring([(sb.alloc([64, 4, 512], BF, "XO"), Tk()) for _ in range(2)])
        pbuf = Ring([(sb.alloc([128, 2, 512], BF, "pbufx"), Tk()) for _ in range(3)])
        xo_r = Ring([(sb.alloc([128, D], F32, "xox"), Tk()) for _ in range(2)])
        def attn(qc):
            xq, xqk = XQT_r.next()
            for hp in range(2):
                b = 6 + hp

                def fn(e, hp=hp, b=b):
                    for k in range(8):
                        ins = e.matmul(bank(b)[:, :], lhsT=Wq[:, k, hp * 128:(hp + 1) * 128], rhs=hT[:, k, 1 + qc * 512:1 + (qc + 1) * 512],
                                       start=(k == 0), stop=(k == 7))
                    return ins
                P.op("pe", fn, r=[wk] + hTk[qc * 4:(qc + 1) * 4], w=[PBK[b]])
                P.op("act", lambda e, hp=hp, b=b: e.activation(out=xq[:, hp, :], in_=bank(b)[:, :], func=AF.Copy, scale=0.125), r=[PBK[b]], w=[xqk])
            xo_t, xo_tk = XO_r.next()

            def s_mm(h):
                pbs = slice(64 * (h % 2), 64 * (h % 2) + 64)
                sd = (h % 2) * 2

                def fn(e):
                    for mt in range(2):
                        ins = e.matmul(bank(sd + mt)[:, :], lhsT=XKT[pbs, h // 2, mt * 128:(mt + 1) * 128], rhs=xq[pbs, h // 2, :], start=True, stop=True)
                    return ins
                P.op("pe", fn, r=[XKTk, xqk], w=[PBK[sd], PBK[sd + 1]])

            s_mm(0)
            stq = []
            for h in range(4):
                sd = (h % 2) * 2
                if h + 1 < 4:
                    s_mm(h + 1)
                pb_, pbk_ = pbuf.next()
                P.op("act", lambda e, pb_=pb_, sd=sd: e.activation(out=pb_[:], in_=PD[sd // 2][:], func=AF.Exp), r=[PBK[sd], PBK[sd + 1]], w=[pbk_])
                ob = 4 + h % 2

                def fn2(e, h=h, pb_=pb_, ob=ob):
                    for mt in range(2):
                        ins = e.matmul(bank(ob)[0:65, :], lhsT=XV[:, mt, h, :], rhs=pb_[:, mt, :], start=(mt == 0), stop=(mt == 1))
                    return ins
                P.op("pe", fn2, r=[XVk, pbk_], w=[PBK[ob]])
                stq.append((h, norm_OT_a(ob, 512)))
                if len(stq) > 1:
                    h0, st0 = stq.pop(0)
                    norm_OT_b(st0, 512, xo_t[:, h0, :], [xo_tk])
            for h0, st0 in stq:
                norm_OT_b(st0, 512, xo_t[:, h0, :], [xo_tk])
            return xo_t, xo_tk

        def outproj(qc, xo_t, xo_tk):
            def bodyO(tt):
                ti = qc * 4 + tt
                xt, xtk = xring.next()
                P.dma("sp", xt[:], XA[r0 + ti * 128:r0 + (ti + 1) * 128, :], r=[XAt[r0 // 128 + ti]], w=[xtk])
                xo, xok = xo_r.next()
                for nh in range(2):
                    b = (tt % 2) * 2 + nh

                    def fn(e, tt=tt, nh=nh, b=b):
                        for h in range(4):
                            ins = e.matmul(bank(b)[:, :], lhsT=xo_t[:, h, tt * 128:(tt + 1) * 128], rhs=Wo[:, h, nh * 512:(nh + 1) * 512],
                                           start=(h == 0), stop=(h == 3))
                        return ins
                    P.op("pe", fn, r=[xo_tk, wk], w=[PBK[b]])
                    P.op("dve", lambda e, xo=xo, xt=xt, nh=nh, b=b: e.tensor_tensor(out=xo[:, nh * 512:(nh + 1) * 512], in0=bank(b)[:, :],
                                                                                    in1=xt[:, nh * 512:(nh + 1) * 512], op=ALU.add), r=[PBK[b], xtk], w=[xok])
                P.dma("pool", XB[r0 + ti * 128:r0 + (ti + 1) * 128, :], xo[:], r=[xok], w=[XBt[r0 // 128 + ti]])
            interleaved(4, bodyO)

        prev = None
        for qc in range(S // 512):
            cur = attn(qc)
            if prev is not None:
                outproj(qc - 1, *prev)
            prev = cur
        outproj(S // 512 - 1, *prev)
        P.barrier()
        sb.release(m0)

    NCH = 22
    CWD = [128] * 21 + [64]

    def ffn(l, si, Wup, Wdn, fcw, wk, last):
        S = seq_lens[si]
        NW = S // 256
        r0 = SOFF[si]
        m0 = sb.mark()
        alloc_aux(2, need_norm=False)
        g, gk = load_gain(Wd["norm_ffn_g"][l])
        if last:
            gf, gfk = load_gain(Wd["final_norm_g"])
        HW = [(sb.alloc([128, 8, 258], BF, "HW"), Tk()) for _ in range(3)]
        AT_r = Ring([(sb.alloc([128, NCH, 256], BF, "AT"), Tk()) for _ in range(1)])
        t0_r = Ring([(sb.alloc([128, 256], F32, "t0F"), Tk()) for _ in range(3)])
        t1_r = Ring([(sb.alloc([128, 256], F32, "t1F"), Tk()) for _ in range(3)])
        xo_r = Ring([(sb.alloc([128, D], F32, "xoF"), Tk()) for _ in range(2)])

        def make_window(w):
            hw, hwk = HW[w % 3]

            def bodyW(j):
                ti = w * 2 + j
                xt, xtk = xring.next()
                P.dma("sp", xt[:], XB[r0 + ti * 128:r0 + (ti + 1) * 128, :], r=[XBt[r0 // 128 + ti]], w=[xtk])
                h, hk = norm_rows(xt, xtk, g, gk)
                transpose_to(h, hk, 8, lambda hw=hw, j=j: hw[:, :, 1 + j * 128:1 + (j + 1) * 128], [hwk])
            interleaved(2, bodyW)

        make_window(0)
        for w in range(NW):
            hw, hwk = HW[w % 3]
            if w + 1 < NW:
                make_window(w + 1)
                hn, hnk = HW[(w + 1) % 3]
                P.op("pool", lambda e, hw=hw, hn=hn: e.tensor_copy(out=hw[:, :, 257:258], in_=hn[:, :, 1:2]), r=[hnk], w=[hwk])
            else:
                P.op("pool", lambda e, hw=hw: e.memset(hw[:, :, 257:258], 0.0), w=[hwk])
            if w > 0:
                hp_, hpk = HW[(w - 1) % 3]
                P.op("pool", lambda e, hw=hw, hp_=hp_: e.tensor_copy(out=hw[:, :, 0:1], in_=hp_[:, :, 256:257]), r=[hpk], w=[hwk])
            else:
                P.op("pool", lambda e, hw=hw: e.memset(hw[:, :, 0:1], 0.0), w=[hwk])
            at, atk = AT_r.next()

            def bodyF(c, hw=hw, hwk=hwk, at=at, atk=atk):
                cwd = CWD[c]
                bg = (c % 3) * 2
                bv = bg + 1
                def fn(e, c=c, cwd=cwd, bg=bg, hw=hw):
                    for k in range(8):
                        ins = e.matmul(bank(bg)[0:cwd, 0:258], lhsT=Wup[:, k, c * 128:c * 128 + cwd], rhs=hw[:, k, 0:258], start=(k == 0), stop=(k == 7))
                    return ins
                P.op("pe", fn, r=[hwk, wk["g"][c // 6]], w=[PBK[bg]])

                def fnv(e, c=c, cwd=cwd, bv=bv, hw=hw):
                    for k in range(8):
                        ins = e.matmul(bank(bv)[0:cwd, 0:256], lhsT=Wup[:, k, DFF + c * 128:DFF + c * 128 + cwd], rhs=hw[:, k, 1:257], start=(k == 0), stop=(k == 7))
                    return ins
                P.op("pe", fnv, r=[hwk, wk["g"][c // 6]], w=[PBK[bv]])
                t0, t0k = t0_r.next()
                t1, t1k = t1_r.next()
                P.op("act", lambda e, t0=t0, bg=bg, c=c, cwd=cwd: e.activation(out=t0[0:cwd, :], in_=bank(bg)[0:cwd, 1:257], func=AF.Identity,
                                                                                scale=fcw[0:cwd, c, 1:2], bias=fcw[0:cwd, c, 3:4]), r=[PBK[bg], wk["c"]], w=[t0k])
                P.op("dve", lambda e, t0=t0, t1=t1, bg=bg, c=c, cwd=cwd: e.scalar_tensor_tensor(out=t1[0:cwd, :], in0=bank(bg)[0:cwd, 0:256], scalar=fcw[0:cwd, c, 0:1],
                                                                                               in1=t0[0:cwd, :], op0=ALU.mult, op1=ALU.add), r=[PBK[bg], wk["c"], t0k], w=[t1k])
                P.op("dve", lambda e, t0=t0, t1=t1, bg=bg, c=c, cwd=cwd: e.scalar_tensor_tensor(out=t0[0:cwd, :], in0=bank(bg)[0:cwd, 2:258], scalar=fcw[0:cwd, c, 2:3],
                                                                                               in1=t1[0:cwd, :], op0=ALU.mult, op1=ALU.add), r=[PBK[bg], wk["c"], t1k], w=[t0k])
                P.op("act", lambda e, t0=t0, t1=t1, cwd=cwd: e.activation(out=t1[0:cwd, :], in_=t0[0:cwd, :], func=AF.Silu), r=[t0k], w=[t1k])
                P.op("dve", lambda e, t1=t1, at=at, bv=bv, c=c, cwd=cwd: e.tensor_tensor(out=at[0:cwd, c, :], in0=t1[0:cwd, :], in1=bank(bv)[0:cwd, 0:256], op=ALU.mult),
                     r=[t1k, PBK[bv]], w=[atk])
            interleaved(NCH, bodyF)
            for tt in range(2):
                ti = w * 2 + tt
                xt, xtk = xring.next()
                P.dma("sp", xt[:], XB[r0 + ti * 128:r0 + (ti + 1) * 128, :], r=[XBt[r0 // 128 + ti]], w=[xtk])
                xo, xok = xo_r.next()
                for nh in range(2):
                    b = 6 + (tt * 2 + nh) % 2
                    def fn(e, at=at, tt=tt, nh=nh, b=b):
                        for c in range(NCH):
                            cwd = CWD[c]
                            ins = e.matmul(bank(b)[:, :], lhsT=at[0:cwd, c, tt * 128:(tt + 1) * 128], rhs=Wdn[0:cwd, c, nh * 512:(nh + 1) * 512],
                                           start=(c == 0), stop=(c == NCH - 1))
                        return ins
                    P.op("pe", fn, r=[atk, wk["d"]], w=[PBK[b]])
                    P.op("dve", lambda e, xo=xo, xt=xt, nh=nh, b=b: e.tensor_tensor(out=xo[:, nh * 512:(nh + 1) * 512], in0=bank(b)[:, :],
                                                                                    in1=xt[:, nh * 512:(nh + 1) * 512], op=ALU.add), r=[PBK[b], xtk], w=[xok])
                if not last:
                    P.dma("pool", XC[r0 + ti * 128:r0 + (ti + 1) * 128, :], xo[:], r=[xok], w=[XCt[r0 // 128 + ti]])
                else:
                    st, stk = stat.next()
                    P.op("dve", lambda e, xo=xo, st=st: e.scalar_tensor_tensor(out=junk[:], in0=xo[:], scalar=1.0, in1=xo[:], op0=ALU.mult, op1=ALU.mult,
                                                                                accum_out=st[:, 0:1]), r=[xok], w=[junkk, stk])
                    rstd_from_ss(st, stk, 1, 1.0 / D)
                    yo, yok = xring.next()
                    P.op("dve", lambda e, xo=xo, st=st, yo=yo: e.scalar_tensor_tensor(out=yo[:], in0=xo[:], scalar=st[:, 1:2], in1=gf[:],
                                                                                       op0=ALU.mult, op1=ALU.mult), r=[xok, stk, gfk], w=[yok])
                    P.dma("pool", ys[si][ti * 128:(ti + 1) * 128, :], yo[:], r=[yok])
        P.barrier()
        sb.release(m0)

    for l in range(depth):
        if "mix" in phases:
            for si in range(NS):
                mixers(l, si)
        if "mixout" in phases:
            m0 = sb.mark()
            wout = sb.alloc([128, 8, D], BF, "wout")
            woutk = Tk()
            P.dma("pool", wout[:], Wd["w_out"][l].rearrange("(k p) n -> p k n", p=128), w=[woutk])
            for si in range(NS):
                mixer_out(l, si, wout, woutk)
            sb.release(m0)
        if "cross" in phases:
            m0 = sb.mark()
            Wq = sb.alloc([128, 8, 256], BF, "Wq")
            Wkv = sb.alloc([128, 8, 512], BF, "Wkv")
            Wo = sb.alloc([64, 4, D], BF, "Wo")
            wk = Tk()
            P.dma("pool", Wq[:], Wd["w_xq"][l].rearrange("(k p) n -> p k n", p=128), w=[wk])
            P.dma("pool", Wkv[:], Wd["w_xkv"][l].rearrange("(k p) n -> p k n", p=128), w=[wk])
            P.dma("pool", Wo[:], Wd["w_xo"][l].rearrange("(h p) n -> p h n", p=64), w=[wk])
            for si in range(NS):
                cross(l, si, Wq, Wkv, Wo, wk)
            sb.release(m0)
        if "ffn" in phases:
            m0 = sb.mark()
            Wup = sb.alloc([128, 8, 2 * DFF], BF, "Wup")
            Wdn = sb.alloc([128, NCH, D], BF, "Wdn")
            fcw = sb.alloc([128, NCH, 4], F32, "fcw")
            wk = {"g": [Tk() for _ in range(4)], "d": Tk(), "c": Tk()}
            for c in range(NCH):
                cwd = CWD[c]
                for j in range(3):
                    P.dma("sp", fcw[0:cwd, c, j:j + 1], Wd["ffn_conv_w"][l, j, c * 128:c * 128 + cwd].rearrange("(p o) -> p o", o=1), w=[wk["c"]])
                P.dma("sp", fcw[0:cwd, c, 3:4], Wd["ffn_conv_b"][l, c * 128:c * 128 + cwd].rearrange("(p o) -> p o", o=1), w=[wk["c"]])
            for g in range(4):
                c0 = 768 * g
                c1 = min(768 * (g + 1), DFF)
                for k in range(8):
                    P.dma("pool", Wup[:, k, c0:c1], Wd["w_ffn_up"][l][k * 128:(k + 1) * 128, c0:c1], w=[wk["g"][g]])
                    P.dma("pool", Wup[:, k, DFF + c0:DFF + c1], Wd["w_ffn_up"][l][k * 128:(k + 1) * 128, DFF + c0:DFF + c1], w=[wk["g"][g]])
            P.dma("pool", Wdn[:, 0:21, :], Wd["w_ffn_down"][l][0:21 * 128, :].rearrange("(c p) n -> p c n", p=128), w=[wk["d"]])
            P.dma("pool", Wdn[0:64, 21, :], Wd["w_ffn_down"][l][21 * 128:DFF, :], w=[wk["d"]])
            for si in range(NS):
                ffn(l, si, Wup, Wdn, fcw, wk, last=(l == depth - 1))
            sb.release(m0)
    P.barrier()
    P.replay()
    return nc, P


_CACHE = {}


def kernel(**inputs):
    seq_lens = (2048, 2048, 4096)
    ncores = 8
    if "nc" not in _CACHE:
        _CACHE["nc"] = build_program(seq_lens)[0]
    nc = _CACHE["nc"]
    consts = host_consts(4096)
    xp = np.asarray(inputs["x_prompt"], dtype=np.float32)
    xsm = np.asarray(inputs["x_sample"], dtype=np.float32)
    mp = np.asarray(inputs["mem_prompt"], dtype=np.float32)
    msm = np.asarray(inputs["mem_sample"], dtype=np.float32)
    in_maps = []
    for c in range(ncores):
        m = {"xs0": np.ascontiguousarray(xp[2 * c]), "xs1": np.ascontiguousarray(xp[2 * c + 1]), "xs2": np.ascontiguousarray(xsm[c]),
             "ms0": np.ascontiguousarray(mp[2 * c]), "ms1": np.ascontiguousarray(mp[2 * c + 1]), "ms2": np.ascontiguousarray(msm[c])}
        for n in WSHAPES:
            m[n] = np.ascontiguousarray(np.asarray(inputs[n], dtype=np.float32))
        m.update(consts)
        in_maps.append(m)
    res = run_bass_kernel_spmd(nc, in_maps, core_ids=list(range(ncores)))
    yp = np.empty((16, 2048, D), np.float32)
    ysm = np.empty((8, 4096, D), np.float32)
    for c in range(ncores):
        r = res.results[c]
        yp[2 * c] = r["ys0"]
        yp[2 * c + 1] = r["ys1"]
        ysm[c] = r["ys2"]
    return (yp, ysm)
```
